# Optimizing a Trainium2 kernel written in Bass

```python
import math
import jax, jax.numpy as jnp
from jax import lax
import numpy as np

D_MODEL = 4096
BATCH = 2
SEQ = 8192
DEPTH = 2

N_MEM = 256
DA_HEAD_DIM = 128
DA_HEADS = D_MODEL // (2 * DA_HEAD_DIM)
Q_BLOCK = 128
N_BUCKETS = 32
MAX_DIST = 128
RW_HEAD = 64
RW_HEADS = D_MODEL // RW_HEAD
RW_DECAY_LORA = max(32, int(round(1.8 * D_MODEL ** 0.5 / 32)) * 32)
RW_AAA_LORA = max(32, int(round(1.8 * D_MODEL ** 0.5 / 32)) * 32)
RW_GATE_LORA = max(32, int(round(0.6 * D_MODEL ** 0.8 / 32)) * 32)
GN_EPS = 64e-5
CA_HEADS = 4
CA_HEAD_DIM = 128
CA_DIM = CA_HEADS * CA_HEAD_DIM
D_FF = ((8 * D_MODEL // 3 + 255) // 256) * 256
CONV_W = 3
N_A = (DEPTH + 1) // 2
N_B = DEPTH // 2
ALPHA = (2 * DEPTH) ** 0.25
BETA = (8 * DEPTH) ** -0.25
LN_EPS = 1e-5

kernel_name = "hybrid_diffattn_rwkv7_memxattn_convglu"


def layer_norm(x, g, b):
    xf = x.astype(jnp.float32)
    mu = jnp.mean(xf, axis=-1, keepdims=True)
    var = jnp.mean(jnp.square(xf - mu), axis=-1, keepdims=True)
    return ((xf - mu) * lax.rsqrt(var + LN_EPS) * g.astype(jnp.float32) + b.astype(jnp.float32)).astype(x.dtype)


def t5_bucket(n):
    max_exact = N_BUCKETS // 2
    nf = jnp.maximum(n, 1).astype(jnp.float32)
    large = max_exact + (jnp.log(nf / max_exact) / math.log(MAX_DIST / max_exact)
                         * (N_BUCKETS - max_exact)).astype(jnp.int32)
    large = jnp.minimum(large, N_BUCKETS - 1)
    return jnp.where(n < max_exact, n, large)


def diff_attention(x, positions, rel_bias, w_qkv, lam_vecs, subln_w, w_o, lam_init):
    B, S, D = x.shape
    H, d = DA_HEADS, DA_HEAD_DIM
    nb = S // Q_BLOCK
    qkv = x @ w_qkv
    q, k, v = jnp.split(qkv, 3, axis=-1)
    q = q.reshape(B, S, H, 2, d) * (d ** -0.5)
    k = k.reshape(B, S, H, 2, d)
    v = v.reshape(B, S, H, 2 * d).transpose(0, 2, 1, 3)
    q1 = q[..., 0, :].transpose(0, 2, 1, 3)
    q2 = q[..., 1, :].transpose(0, 2, 1, 3)
    k1 = k[..., 0, :].transpose(0, 2, 1, 3)
    k2 = k[..., 1, :].transpose(0, 2, 1, 3)
    lv = lam_vecs.astype(jnp.float32)
    lam = jnp.exp(jnp.sum(lv[0] * lv[1])) - jnp.exp(jnp.sum(lv[2] * lv[3])) + lam_init
    bias_table = rel_bias.astype(jnp.float32).T
    neg = jnp.finfo(jnp.float32).min

    def to_blocks(t):
        return jnp.moveaxis(t.reshape(B, H, nb, Q_BLOCK, t.shape[-1]), 2, 0)

    qpos = jnp.moveaxis(positions.reshape(B, nb, Q_BLOCK), 1, 0)

    def block(args):
        q1b, q2b, qp = args
        n = qp[:, :, None] - positions[:, None, :]
        mask = (n >= 0)[:, None]
        bias = jnp.take(bias_table, t5_bucket(jnp.maximum(n, 0)), axis=1).transpose(1, 0, 2, 3)

        def probs(qb, kb):
            s = jnp.einsum('bhqd,bhkd->bhqk', qb, kb).astype(jnp.float32) + bias
            return jax.nn.softmax(jnp.where(mask, s, neg), axis=-1)

        a = probs(q1b, k1) - lam * probs(q2b, k2)
        return jnp.einsum('bhqk,bhkv->bhqv', a.astype(v.dtype), v)

    o = lax.map(block, (to_blocks(q1), to_blocks(q2), qpos))
    o = o.transpose(1, 0, 3, 2, 4).reshape(B, S, H, 2 * d).astype(jnp.float32)
    o = o * lax.rsqrt(jnp.mean(jnp.square(o), axis=-1, keepdims=True) + LN_EPS)
    o = o * subln_w.astype(jnp.float32) * (1.0 - lam_init)
    return o.reshape(B, S, D).astype(x.dtype) @ w_o


def rwkv7_step(state, inp):
    r_t, w_t, k_t, v_t, a_t, b_t = inp
    sa = jnp.einsum('bhvk,bhk->bhv', state, a_t)
    state = (state * w_t[:, :, None, :] + sa[..., None] * b_t[:, :, None, :]
             + v_t[..., None] * k_t[:, :, None, :])
    y = jnp.einsum('bhvk,bhk->bhv', state, r_t)
    return state, y


def rwkv7_time_mix(x, mix, w_rkv, w0, w1, w2, a0, a1, a2, g1, g2, k_k, k_a, r_k, lnx_w, lnx_b, w_o):
    B, S, D = x.shape
    H, N = RW_HEADS, RW_HEAD
    f32 = jnp.float32
    xx = jnp.pad(x, ((0, 0), (1, 0), (0, 0)))[:, :-1] - x
    xr = x + xx * mix[0]
    xw = x + xx * mix[1]
    xk = x + xx * mix[2]
    xv = x + xx * mix[3]
    xa = x + xx * mix[4]
    xg = x + xx * mix[5]
    r = (xr @ w_rkv[0]).astype(f32)
    k = (xk @ w_rkv[1]).astype(f32)
    v = (xv @ w_rkv[2]).astype(f32)
    w_log = -jax.nn.softplus(-(w0 + jnp.tanh(xw @ w1) @ w2).astype(f32)) - 0.5
    decay = jnp.exp(-jnp.exp(w_log))
    a = jax.nn.sigmoid((a0 + (xa @ a1) @ a2).astype(f32))
    g = (jax.nn.sigmoid(xg @ g1) @ g2).astype(f32)
    heads = lambda t: t.reshape(B, S, H, N)
    kk = heads(k * k_k.astype(f32))
    kk = kk / jnp.maximum(jnp.linalg.norm(kk, axis=-1, keepdims=True), 1e-12)
    k = k * (1.0 + (a - 1.0) * k_a.astype(f32))
    seq_first = lambda t: jnp.moveaxis(t, 1, 0)
    inputs = (seq_first(heads(r)), seq_first(heads(decay)), seq_first(heads(k)),
              seq_first(heads(v)), seq_first(-kk), seq_first(kk * heads(a)))
    state0 = jnp.zeros((B, H, N, N), f32)
    _, y = lax.scan(rwkv7_step, state0, inputs)
    y = jnp.moveaxis(y, 0, 1)
    mu = jnp.mean(y, axis=-1, keepdims=True)
    var = jnp.mean(jnp.square(y - mu), axis=-1, keepdims=True)
    yn = ((y - mu) * lax.rsqrt(var + GN_EPS)).reshape(B, S, D) * lnx_w.astype(f32) + lnx_b.astype(f32)
    bonus = jnp.sum(heads(r) * heads(k) * r_k.astype(f32), axis=-1, keepdims=True) * heads(v)
    out = (yn + bonus.reshape(B, S, D)) * g
    return out.astype(x.dtype) @ w_o


def memory_cross_attention(x, mem, w_q, w_kv, w_o):
    B, S, _ = x.shape
    M = mem.shape[1]
    q = (x @ w_q).reshape(B, S, CA_HEADS, CA_HEAD_DIM) * (CA_HEAD_DIM ** -0.5)
    kv = (mem @ w_kv).reshape(B, M, 2, CA_HEADS, CA_HEAD_DIM)
    s = jnp.einsum('bshd,bmhd->bhsm', q, kv[:, :, 0]).astype(jnp.float32)
    p = jax.nn.softmax(s, axis=-1)
    o = jnp.einsum('bhsm,bmhd->bshd', p.astype(x.dtype), kv[:, :, 1]).reshape(B, S, CA_DIM)
    return o @ w_o


def conv_glu_ffn(x, w_up, conv_w, conv_b, w_down):
    h = x @ w_up
    C = h.shape[-1]
    h = lax.conv_general_dilated(h, conv_w[:, None, :].astype(h.dtype), window_strides=(1,),
                                 padding=[(CONV_W - 1, 0)], dimension_numbers=('NWC', 'WIO', 'NWC'),
                                 feature_group_count=C) + conv_b
    gate, val = jnp.split(h, 2, axis=-1)
    return (jax.nn.silu(gate) * val) @ w_down


def setup_inputs(seed: int = 0) -> dict:
    key = jax.random.key(seed)
    ks = iter(jax.random.split(key, 40))
    nrm = lambda shape, scale: jax.random.normal(next(ks), shape, jnp.float32) * scale
    uni = lambda shape, lo, hi: jax.random.uniform(next(ks), shape, jnp.float32, lo, hi)
    D, F = D_MODEL, D_FF
    x = nrm((BATCH, SEQ, D), 1.0)
    mem = nrm((BATCH, N_MEM, D), 1.0)
    offset = jax.random.randint(next(ks), (BATCH, 1), 0, 4096, dtype=jnp.int32)
    positions = offset + jnp.arange(SEQ, dtype=jnp.int32)[None, :]
    return {
        'x': x,
        'mem': mem,
        'positions': positions,
        'rel_bias': nrm((N_BUCKETS, DA_HEADS), 0.5),
        'da_w_qkv': nrm((N_A, D, 3 * D), D ** -0.5),
        'da_lam': nrm((N_A, 4, DA_HEAD_DIM), 0.1),
        'da_subln': 1.0 + nrm((N_A, 2 * DA_HEAD_DIM), 0.02),
        'da_w_o': nrm((N_A, D, D), D ** -0.5 * BETA),
        'rw_mix': uni((N_B, 6, D), 0.0, 1.0),
        'rw_w_rkv': nrm((N_B, 3, D, D), D ** -0.5),
        'rw_w0': uni((N_B, D), -6.0, -1.0),
        'rw_w1': nrm((N_B, D, RW_DECAY_LORA), D ** -0.5),
        'rw_w2': nrm((N_B, RW_DECAY_LORA, D), 0.1 * RW_DECAY_LORA ** -0.5),
        'rw_a0': nrm((N_B, D), 0.1),
        'rw_a1': nrm((N_B, D, RW_AAA_LORA), D ** -0.5),
        'rw_a2': nrm((N_B, RW_AAA_LORA, D), 0.1 * RW_AAA_LORA ** -0.5),
        'rw_g1': nrm((N_B, D, RW_GATE_LORA), D ** -0.5),
        'rw_g2': nrm((N_B, RW_GATE_LORA, D), RW_GATE_LORA ** -0.5),
        'rw_k_k': 0.85 + nrm((N_B, D), 0.05),
        'rw_k_a': 1.0 + nrm((N_B, D), 0.05),
        'rw_r_k': nrm((N_B, RW_HEADS, RW_HEAD), 0.1),
        'rw_lnx_w': 1.0 + nrm((N_B, D), 0.02),
        'rw_lnx_b': nrm((N_B, D), 0.02),
        'rw_w_o': nrm((N_B, D, D), D ** -0.5 * BETA),
        'ca_w_q': nrm((DEPTH, D, CA_DIM), D ** -0.5),
        'ca_w_kv': nrm((DEPTH, D, 2 * CA_DIM), D ** -0.5),
        'ca_w_o': nrm((DEPTH, CA_DIM, D), CA_DIM ** -0.5 * BETA),
        'ffn_w_up': nrm((DEPTH, D, 2 * F), D ** -0.5),
        'ffn_conv_w': nrm((DEPTH, CONV_W, 2 * F), CONV_W ** -0.5),
        'ffn_conv_b': nrm((DEPTH, 2 * F), 0.02),
        'ffn_w_down': nrm((DEPTH, F, D), F ** -0.5 * BETA),
        'ln_g': 1.0 + nrm((DEPTH, 3, D), 0.02),
        'ln_b': nrm((DEPTH, 3, D), 0.02),
    }


def reference(x, mem, positions, rel_bias, da_w_qkv, da_lam, da_subln, da_w_o,
              rw_mix, rw_w_rkv, rw_w0, rw_w1, rw_w2, rw_a0, rw_a1, rw_a2, rw_g1, rw_g2,
              rw_k_k, rw_k_a, rw_r_k, rw_lnx_w, rw_lnx_b, rw_w_o,
              ca_w_q, ca_w_kv, ca_w_o, ffn_w_up, ffn_conv_w, ffn_conv_b, ffn_w_down,
              ln_g, ln_b):
    for i in range(DEPTH):
        j = i // 2
        if i % 2 == 0:
            lam_init = 0.8 - 0.6 * math.exp(-0.3 * i)
            h = diff_attention(x, positions, rel_bias, da_w_qkv[j], da_lam[j], da_subln[j], da_w_o[j], lam_init)
        else:
            h = rwkv7_time_mix(x, rw_mix[j], rw_w_rkv[j], rw_w0[j], rw_w1[j], rw_w2[j],
                               rw_a0[j], rw_a1[j], rw_a2[j], rw_g1[j], rw_g2[j],
                               rw_k_k[j], rw_k_a[j], rw_r_k[j], rw_lnx_w[j], rw_lnx_b[j], rw_w_o[j])
        x = layer_norm(ALPHA * x + h, ln_g[i, 0], ln_b[i, 0])
        x = layer_norm(ALPHA * x + memory_cross_attention(x, mem, ca_w_q[i], ca_w_kv[i], ca_w_o[i]),
                       ln_g[i, 1], ln_b[i, 1])
        x = layer_norm(ALPHA * x + conv_glu_ffn(x, ffn_w_up[i], ffn_conv_w[i], ffn_conv_b[i], ffn_w_down[i]),
                       ln_g[i, 2], ln_b[i, 2])
    return x
```

```python
import numpy as np
import concourse.bass as bass
import concourse.mybir as mybir

F32 = mybir.dt.float32
BF16 = mybir.dt.bfloat16
I32 = mybir.dt.int32
AF = mybir.ActivationFunctionType
ALU = mybir.AluOpType
AX = mybir.AxisListType


class T:
    __slots__ = ("name", "h", "writer", "readers", "dsem", "dcnt", "p")

    def __init__(self, p, name, h):
        self.p = p
        self.name = name
        self.h = h
        self.writer = None
        self.readers = {}
        self.dsem = None
        self.dcnt = 0

    def __getitem__(self, k):
        return self.h[k]

    def sem(self):
        if self.dsem is None:
            self.dsem = self.p.new_sem("d_" + self.name)
        return self.dsem


class Eng:
    def __init__(self, p, name, h, selfsync):
        self.p = p
        self.name = name
        self.h = h
        self.sem = p.new_sem("e_" + name)
        self.count = 0
        self.known = {}
        self.pending = False
        self.selfsync = selfsync
        self.ninst = 0


class P:
    def __init__(self, nc):
        self.nc = nc
        self.nsem = 0
        self.pe = Eng(self, "pe", nc.tensor, False)
        self.act = Eng(self, "act", nc.scalar, True)
        self.dve = Eng(self, "dve", nc.vector, True)
        self.pool = Eng(self, "pool", nc.gpsimd, True)
        self.sp = Eng(self, "sp", nc.sync, False)
        self.engs = [self.pe, self.act, self.dve, self.pool, self.sp]
        self.ntile = 0

    def new_sem(self, name):
        self.nsem += 1
        return self.nc.alloc_semaphore(name)

    def sb(self, name, shape, dt):
        return T(self, name, self.nc.alloc_sbuf_tensor("s_" + name, list(shape), dt))

    def ps(self, name, shape, dt=F32):
        return T(self, name, self.nc.alloc_psum_tensor("p_" + name, list(shape), dt))

    def view(self, name, ap):
        return T(self, name, ap)

    def _need(self, ins, outs):
        need = {}

        def add(ev):
            if ev is None:
                return
            s, v = ev
            k = id(s)
            if k not in need or need[k][1] < v:
                need[k] = (s, v)

        for t in ins:
            add(t.writer)
        for t in outs:
            add(t.writer)
            for ev in t.readers.values():
                add(ev)
        return need

    def _wait(self, e, need):
        for k, (s, v) in need.items():
            if s is e.sem and not e.selfsync:
                continue
            if e.known.get(k, 0) >= v:
                continue
            e.h.wait_ge(s, v)
            e.ninst += 1
            e.known[k] = v

    def _mark(self, ins, outs, ev):
        k = id(ev[0])
        for t in ins:
            old = t.readers.get(k)
            if old is None or old[1] < ev[1]:
                t.readers[k] = ev
        for t in outs:
            t.writer = ev
            t.readers = {}

    def op(self, e, fn, ins=(), outs=(), signal=True):
        need = self._need(ins, outs)
        self._wait(e, need)
        inst = fn(e.h)
        e.ninst += 1
        if signal:
            e.count += 1
            inst.then_inc(e.sem, 1)
            e.pending = False
            ev = (e.sem, e.count)
        else:
            e.pending = True
            ev = (e.sem, e.count + 1)
        self._mark(ins, outs, ev)
        return ev

    def dma(self, q, out_ap, in_ap, ins=(), outs=(), owner=None):
        need = self._need(ins, outs)
        self._wait(q, need)
        if owner is None:
            owner = outs[0] if outs else ins[0]
        s = owner.sem()
        inst = q.h.dma_start(out=out_ap, in_=in_ap)
        q.ninst += 1
        owner.dcnt += 16
        inst.then_inc(s, 16)
        ev = (s, owner.dcnt)
        self._mark(ins, outs, ev)
        return ev

    def finish(self, tiles):
        for e in self.engs:
            assert not e.pending, e.name
        need = self._need(tiles, tiles)
        self._wait(self.sp, need)
        for e in self.engs:
            if e.count and e is not self.sp:
                self.sp.h.wait_ge(e.sem, e.count)

    def stats(self):
        return {e.name: e.ninst for e in self.engs} | {"nsem": self.nsem}


D = 4096
KC = 32
TT = 410
NT = 5
NCOL = NT * TT
FC = 86
FPARTS = [(0, 22), (22, 44), (44, 65), (65, 86)]
NMEM = 256
ALPHA = float(4 ** 0.25)
LN_EPS = 1e-5
NB = 5


class Dummy:
    def __getitem__(self, k):
        return self

    def rearrange(self, *a, **k):
        return self


class DryP:
    dry = True

    def __init__(self):
        self.pe = self.act = self.dve = self.pool = self.sp = None

    def sb(self, *a):
        return Dummy()

    ps = sb
    view = sb

    def op(self, *a, **k):
        pass

    dma = op


class WStream:
    def __init__(self, p, nb, seq=None):
        self.p = p
        self.nb = nb
        self.record = seq is None
        self.seq = [] if seq is None else seq
        self.cur = 0
        self.loaded = 0
        self.bufs = [p.sb(f"wb{i}", [128, 32 * 128], BF16) for i in range(nb)]

    def next(self, ap, n):
        if self.record:
            self.seq.append((ap, n))
            return self.bufs[0]
        p = self.p
        i = self.cur
        self.cur += 1
        while self.loaded < min(len(self.seq), i + self.nb):
            j = self.loaded
            apj, nj = self.seq[j]
            b = self.bufs[j % self.nb]
            p.dma(p.pool, b[:, 0:nj * 128], apj.rearrange("p n j -> p (n j)"), outs=[b])
            self.loaded += 1
        return self.bufs[i % self.nb]


class Rot:
    def __init__(self, tiles):
        self.t = tiles
        self.i = 0

    def get(self):
        t = self.t[self.i % len(self.t)]
        self.i += 1
        return t


def declare_pm(nc):
    d = {}

    def inp(name, shape, dt=F32):
        d[name] = nc.dram_tensor(name, list(shape), dt, kind="ExternalInput").ap()

    inp("mixT", [128, KC, NCOL], BF16)
    inp("xT", [128, KC, NCOL])
    inp("hmask", [128, 1])
    inp("wo", [32, 128, 32, 128])
    inp("caq", [4, 128, 32, 128])
    inp("cak", [4, 128, 32, 128])
    inp("cav", [4, 128, 32, 128])
    inp("cao", [32, 128, 4, 128])
    inp("memT", [128, KC, NMEM])
    inp("wupg", [FC, 128, 32, 128])
    inp("wupv", [FC, 128, 32, 128])
    inp("wdn", [32, 128, FC, 128])
    inp("convw", [128, FC, 2, 3])
    inp("convb", [128, FC, 2])
    inp("lng", [128, 3, KC])
    inp("lnb", [128, 3, KC])
    d["yT"] = nc.dram_tensor("yT", [128, KC, NCOL], F32, kind="ExternalOutput").ap()
    return d


def emit_pm(p, d, ws, ntiles=NT):
    W = TT
    xres_all = p.sb("xres", [128, KC, W], F32)
    xbf_all = p.sb("xbf", [128, KC, W], BF16)
    xres = [p.view(f"xres{k}", xres_all[:, k, :]) for k in range(KC)]
    xbf = [p.view(f"xbf{k}", xbf_all[:, k, :]) for k in range(KC)]
    gpart = p.sb("gpart", [128, 22 * W], BF16)
    kT = p.sb("kT", [128, 4, NMEM], BF16)
    vm = p.sb("vm", [128, 2, 4, 128], BF16)
    qT = p.sb("qT", [128, 4, W], BF16)
    oT = p.sb("oT", [128, 4, W], BF16)
    PT = Rot([p.sb(f"PT{i}", [128, 2, W], BF16) for i in range(2)])
    halo = p.sb("halo", [128, FC * 2, 2], F32)
    convw = p.sb("convw", [128, FC, 2, 3], F32)
    convb = p.sb("convb", [128, FC, 2], F32)
    lng = p.sb("lng", [128, 3, KC], F32)
    lnb = p.sb("lnb", [128, 3, KC], F32)
    hmask = p.sb("hmask", [128, 1], F32)
    onesf = p.sb("onesf", [128, 128], F32)
    onesb = p.sb("onesb", [128, 128], BF16)
    psA = Rot([p.ps(f"psA{i}", [128, 512]) for i in range(6)])
    psB = Rot([p.ps(f"psB{i}", [128, 512]) for i in range(2)])
    hb = Rot([p.sb(f"hb{i}", [128, W + 2], F32) for i in range(4)])
    tmp = Rot([p.sb(f"tmp{i}", [128, W], F32) for i in range(8)])
    stat = Rot([p.sb(f"stat{i}", [128, W], F32) for i in range(4)])

    p.dma(p.sp, convw[:], d["convw"], outs=[convw])
    p.dma(p.sp, convb[:], d["convb"], outs=[convb])
    p.dma(p.sp, lng[:], d["lng"], outs=[lng])
    p.dma(p.sp, lnb[:], d["lnb"], outs=[lnb])
    p.dma(p.sp, hmask[:], d["hmask"], outs=[hmask])
    p.op(p.dve, lambda h: h.memset(onesf[:], 1.0 / D), outs=[onesf])
    p.op(p.dve, lambda h: h.memset(onesb[:], 1.0), outs=[onesb])

    def mm_group(ps, n, lhs, rhs, width, fp32=False):
        for k in range(n):
            p.op(p.pe, lambda h, k=k: h.matmul(ps[:, 0:width], lhs(k)[0], rhs(k)[0], start=(k == 0), stop=(k == n - 1)),
                 ins=[lhs(k)[1], rhs(k)[1]], outs=[ps], signal=(k == n - 1))

    memT = gpart
    p.dma(p.pool, memT[:, 0:KC * NMEM], d["memT"].rearrange("p k m -> p (k m)"), outs=[memT])
    for hh in range(4):
        wb = ws.next(d["cak"][hh], 32)
        ps = psA.get()
        mm_group(ps, KC, lambda k: (wb[:, k * 128:(k + 1) * 128], wb),
                 lambda k: (memT[:, k * NMEM:(k + 1) * NMEM], memT), NMEM)
        p.op(p.act, lambda h: h.activation(kT[:, hh, :], ps[:, 0:NMEM], AF.Copy), ins=[ps], outs=[kT])
    for hh in range(4):
        wb = ws.next(d["cav"][hh], 32)
        for mc in range(2):
            ps = psA.get()
            mm_group(ps, KC, lambda k: (memT[:, k * NMEM + mc * 128:k * NMEM + (mc + 1) * 128], memT),
                     lambda k: (wb[:, k * 128:(k + 1) * 128], wb), 128)
            p.op(p.act, lambda h: h.activation(vm[:, mc, hh, :], ps[:, 0:128], AF.Copy), ins=[ps], outs=[vm])

    def evac_resid(ps, m, first=True):
        if first:
            p.op(p.dve, lambda h: h.scalar_tensor_tensor(xres[m][:], xres[m][:], ALPHA, ps[:, 0:W], ALU.mult, ALU.add),
                 ins=[ps, xres[m]], outs=[xres[m]])
        else:
            p.op(p.dve, lambda h: h.tensor_tensor(xres[m][:], xres[m][:], ps[:, 0:W], ALU.add),
                 ins=[ps, xres[m]], outs=[xres[m]])

    def layer_norm(idx, want_bf=True):
        psm = psB.get()
        mm_group(psm, KC, lambda k: (onesf[:], onesf), lambda k: (xres[k][:], xres[k]), W)
        mean = stat.get()
        p.op(p.act, lambda h: h.activation(mean[:], psm[:, 0:W], AF.Copy), ins=[psm], outs=[mean])
        psv = psB.get()
        for k in range(KC):
            p.op(p.dve, lambda h, k=k: h.tensor_tensor(xres[k][:], xres[k][:], mean[:], ALU.subtract),
                 ins=[xres[k], mean], outs=[xres[k]])
            sq = tmp.get()
            p.op(p.act, lambda h, k=k, sq=sq: h.activation(sq[:], xres[k][:], AF.Square), ins=[xres[k]], outs=[sq])
            p.op(p.pe, lambda h, k=k, sq=sq: h.matmul(psv[:, 0:W], onesf[:], sq[:], start=(k == 0), stop=(k == KC - 1)),
                 ins=[onesf, sq], outs=[psv], signal=True)
        rstd = stat.get()
        p.op(p.dve, lambda h: h.tensor_scalar(rstd[:], psv[:, 0:W], LN_EPS, None, ALU.add), ins=[psv], outs=[rstd])
        p.op(p.act, lambda h: h.activation(rstd[:], rstd[:], AF.Sqrt), ins=[rstd], outs=[rstd])
        p.op(p.dve, lambda h: h.reciprocal(rstd[:], rstd[:]), ins=[rstd], outs=[rstd])
        for k in range(KC):
            p.op(p.dve, lambda h, k=k: h.tensor_tensor(xres[k][:], xres[k][:], rstd[:], ALU.mult),
                 ins=[xres[k], rstd], outs=[xres[k]])
            p.op(p.pool, lambda h, k=k: h.tensor_scalar(xres[k][:], xres[k][:], lng[:, idx, k:k + 1], lnb[:, idx, k:k + 1],
                                                       ALU.mult, ALU.add),
                 ins=[xres[k], lng, lnb], outs=[xres[k]])
            if want_bf:
                p.op(p.act, lambda h, k=k: h.activation(xbf[k][:], xres[k][:], AF.Copy), ins=[xres[k]], outs=[xbf[k]])

    xTv = d["xT"]
    mixv = d["mixT"]
    yv = d["yT"]
    for ti in range(ntiles):
        c0 = ti * W
        p.dma(p.sp, xres_all[:], xTv[:, :, c0:c0 + W], outs=xres)
        p.dma(p.sp, xbf_all[:], mixv[:, :, c0:c0 + W], outs=xbf)
        for m in range(KC):
            wb = ws.next(d["wo"][m], 32)
            ps = psA.get()
            mm_group(ps, KC, lambda k: (wb[:, k * 128:(k + 1) * 128], wb), lambda k: (xbf[k][:], xbf[k]), W)
            evac_resid(ps, m)
        layer_norm(0)
        for hh in range(4):
            wb = ws.next(d["caq"][hh], 32)
            ps = psA.get()
            mm_group(ps, KC, lambda k: (wb[:, k * 128:(k + 1) * 128], wb), lambda k: (xbf[k][:], xbf[k]), W)
            p.op(p.act, lambda h: h.activation(qT[:, hh, :], ps[:, 0:W], AF.Copy), ins=[ps], outs=[qT])
        for hh in range(4):
            pt = PT.get()
            for mc in range(2):
                ps = psA.get()
                p.op(p.pe, lambda h: h.matmul(ps[:, 0:W], kT[:, hh, mc * 128:(mc + 1) * 128], qT[:, hh, :], start=True, stop=True),
                     ins=[kT, qT], outs=[ps])
                p.op(p.act, lambda h: h.activation(pt[:, mc, :], ps[:, 0:W], AF.Exp, scale=float(128 ** -0.5)),
                     ins=[ps], outs=[pt])
            psl = psB.get()
            mm_group(psl, 2, lambda k: (onesb[:], onesb), lambda k: (pt[:, k, :], pt), W)
            rl = stat.get()
            p.op(p.dve, lambda h: h.reciprocal(rl[:], psl[:, 0:W]), ins=[psl], outs=[rl])
            pso = psA.get()
            mm_group(pso, 2, lambda k: (vm[:, k, hh, :], vm), lambda k: (pt[:, k, :], pt), W)
            p.op(p.dve, lambda h: h.tensor_tensor(oT[:, hh, :], pso[:, 0:W], rl[:], ALU.mult), ins=[pso, rl], outs=[oT])
        for m in range(KC):
            wb = ws.next(d["cao"][m], 4)
            ps = psA.get()
            mm_group(ps, 4, lambda k: (wb[:, k * 128:(k + 1) * 128], wb), lambda k: (oT[:, k, :], oT), W)
            evac_resid(ps, m)
        layer_norm(1)
        for pi, (f0, f1) in enumerate(FPARTS):
            for j in range(f0, f1):
                tcv = []
                for half, wname in ((0, "wupg"), (1, "wupv")):
                    wb = ws.next(d[wname][j], 32)
                    ps = psA.get()
                    mm_group(ps, KC, lambda k: (wb[:, k * 128:(k + 1) * 128], wb), lambda k: (xbf[k][:], xbf[k]), W)
                    hbuf = hb.get()
                    p.op(p.act, lambda h: h.activation(hbuf[:, 2:W + 2], ps[:, 0:W], AF.Copy), ins=[ps], outs=[hbuf])
                    hi = j * 2 + half
                    if ti == 0:
                        p.op(p.pool, lambda h: h.memset(hbuf[:, 0:2], 0.0), outs=[hbuf])
                        p.op(p.pool, lambda h: h.tensor_scalar(hbuf[:, 2:4], hbuf[:, 2:4], hmask[:, 0:1], None, ALU.mult),
                             ins=[hmask, hbuf], outs=[hbuf])
                    else:
                        p.op(p.pool, lambda h: h.tensor_copy(hbuf[:, 0:2], halo[:, hi, :]), ins=[halo], outs=[hbuf])
                    if ti < ntiles - 1:
                        p.op(p.pool, lambda h: h.tensor_copy(halo[:, hi, :], hbuf[:, W:W + 2]), ins=[hbuf], outs=[halo])
                    t = tmp.get()
                    p.op(p.dve, lambda h: h.tensor_scalar(t[:], hbuf[:, 2:W + 2], convw[:, j, half, 2:3], convb[:, j, half:half + 1],
                                                         ALU.mult, ALU.add), ins=[hbuf, convw, convb], outs=[t])
                    p.op(p.dve, lambda h: h.scalar_tensor_tensor(t[:], hbuf[:, 1:W + 1], convw[:, j, half, 1:2], t[:], ALU.mult, ALU.add),
                         ins=[hbuf, convw, t], outs=[t])
                    p.op(p.dve, lambda h: h.scalar_tensor_tensor(t[:], hbuf[:, 0:W], convw[:, j, half, 0:1], t[:], ALU.mult, ALU.add),
                         ins=[hbuf, convw, t], outs=[t])
                    tcv.append(t)
                sg = tmp.get()
                p.op(p.act, lambda h: h.activation(sg[:], tcv[0][:], AF.Silu), ins=[tcv[0]], outs=[sg])
                jj = j - f0
                p.op(p.pool, lambda h: h.tensor_tensor(gpart[:, jj * W:(jj + 1) * W], sg[:], tcv[1][:], ALU.mult),
                     ins=[sg, tcv[1]], outs=[gpart])
            nf = f1 - f0
            for m in range(KC):
                wb = ws.next(d["wdn"][m][:, f0:f1, :], nf)
                ps = psA.get()
                mm_group(ps, nf, lambda k: (wb[:, k * 128:(k + 1) * 128], wb), lambda k: (gpart[:, k * W:(k + 1) * W], gpart), W)
                evac_resid(ps, m, first=(pi == 0))
        layer_norm(2, want_bf=False)
        p.dma(p.sp, yv[:, :, c0:c0 + W], xres_all[:], ins=xres, owner=xres[0])
    return xres


def build_pm(ntiles=NT):
    nc = bass.Bass("TRN2", target_bir_lowering=False)
    d = declare_pm(nc)
    dry = DryP()
    ws0 = WStream(dry, NB)
    emit_pm(dry, d, ws0, ntiles)
    p = P(nc)
    ws = WStream(p, NB, seq=ws0.seq)
    xres = emit_pm(p, d, ws, ntiles)
    p.finish(xres)
    print("pm stats", p.stats(), "panels", len(ws0.seq))
    return nc


import math

SEQ = 8192
QT = 256
LAM_INIT0 = 0.8 - 0.6 * math.exp(0.0)
SCALE = float(128 ** -0.5)
NEG = -30000.0
THR = [0] + list(range(1, 17)) + [int(math.ceil(16 * 8 ** ((b - 16) / 16.0))) for b in range(17, 32)]


def declare_att(nc, ntok):
    d = {}

    def inp(name, shape, dt=F32):
        d[name] = nc.dram_tensor(name, list(shape), dt, kind="ExternalInput").ap()

    inp("xT", [128, KC, ntok])
    inp("wqkv", [2, 128, KC, 768])
    inp("relb", [128, 32, 2])
    inp("lam", [128, 4, 128])
    inp("subw", [128, 256])
    inp("nidx", [128, 512])
    d["o"] = nc.dram_tensor("o", [2, ntok, 256], BF16, kind="ExternalOutput").ap()
    return d


def emit_att(p, d, seq, nbatch=2):
    nqt = seq // QT
    nblk = seq // 128
    W = [p.sb(f"W{h}", [128, KC, 768], BF16) for h in range(2)]
    KT = p.sb("KT", [128, 2, seq], BF16)
    Vc = p.sb("Vc", [128, nblk, 257], BF16)
    xch = p.sb("xch", [128, KC, QT], BF16)
    qT = p.sb("qT", [128, 2, QT], BF16)
    relb = p.sb("relb", [128, 32, 2], F32)
    dtab = p.sb("dtab", [128, 31, 2], F32)
    lamr = p.sb("lamr", [128, 4, 128], F32)
    subw = p.sb("subw", [128, 256], F32)
    nidx = p.sb("nidx", [128, 512], F32)
    biasall = [p.sb(f"bias{h}", [128, 512], F32) for h in range(2)]
    b31 = p.sb("b31", [128, 2], F32)
    neglam = p.sb("neglam", [128, 1], F32)
    small = p.sb("small", [128, 8], F32)
    psS = Rot([p.ps(f"psS{i}", [128, 512]) for i in range(4)])
    acc = [[p.ps(f"acc{m}{q}", [128, 512]) for q in range(2)] for m in range(2)]
    pts = Rot([p.sb(f"pt{i}", [128, QT], BF16) for i in range(4)])
    tmpf = Rot([p.sb(f"tf{i}", [128, 512], F32) for i in range(4)])
    sm = Rot([p.sb(f"sm{i}", [128, 4], F32) for i in range(8)])
    ob = Rot([p.sb(f"ob{i}", [128, 256], BF16) for i in range(2)])

    for h in range(2):
        for part in range(3):
            p.dma(p.pool, W[h][:, :, part * 256:(part + 1) * 256], d["wqkv"][h][:, :, part * 256:(part + 1) * 256], outs=[W[h]])
    p.dma(p.sp, relb[:], d["relb"], outs=[relb])
    p.dma(p.sp, lamr[:], d["lam"], outs=[lamr])
    p.dma(p.sp, subw[:], d["subw"], outs=[subw])
    p.dma(p.sp, nidx[:], d["nidx"], outs=[nidx])
    p.op(p.dve, lambda h: h.memset(Vc[:, :, 256:257], 1.0), outs=[Vc])
    p.op(p.dve, lambda h: h.tensor_tensor(dtab[:], relb[:, 1:32, :], relb[:, 0:31, :], ALU.subtract), ins=[relb], outs=[dtab])
    p.op(p.dve, lambda h: h.tensor_copy(b31[:], relb[:, 31, :]), ins=[relb], outs=[b31])
    for i in range(2):
        t = tmpf.get()
        p.op(p.dve, lambda h: h.tensor_tensor(t[:, 0:128], lamr[:, 2 * i, :], lamr[:, 2 * i + 1, :], ALU.mult), ins=[lamr], outs=[t])
        p.op(p.dve, lambda h: h.tensor_reduce(small[:, i:i + 1], t[:, 0:128], AX.X, ALU.add), ins=[t], outs=[small])
    p.op(p.act, lambda h: h.activation(small[:, 2:4], small[:, 0:2], AF.Exp), ins=[small], outs=[small])
    p.op(p.dve, lambda h: h.tensor_tensor(small[:, 4:5], small[:, 3:4], small[:, 2:3], ALU.subtract), ins=[small], outs=[small])
    p.op(p.dve, lambda h: h.tensor_scalar(neglam[:], small[:, 4:5], -LAM_INIT0, None, ALU.add), ins=[small], outs=[neglam])
    for hh in range(2):
        ba = biasall[hh]
        p.op(p.dve, lambda h: h.tensor_scalar(ba[:], nidx[:], 0.0, NEG, ALU.is_lt, ALU.mult), ins=[nidx], outs=[ba])
        p.op(p.dve, lambda h: h.tensor_scalar(ba[:], ba[:], relb[:, 0, hh:hh + 1], None, ALU.add), ins=[ba, relb], outs=[ba])
        for b in range(1, 32):
            t = tmpf.get()
            p.op(p.dve, lambda h: h.tensor_scalar(t[:], nidx[:], float(THR[b]), dtab[:, b - 1, hh:hh + 1], ALU.is_ge, ALU.mult),
                 ins=[nidx, dtab], outs=[t])
            p.op(p.pool, lambda h: h.tensor_tensor(ba[:], ba[:], t[:], ALU.add), ins=[ba, t], outs=[ba])

    xv = d["xT"]
    for b in range(nbatch):
        for hh in range(2):
            Wh = W[hh]
            for qt in range(nqt):
                tok0 = b * seq + qt * QT
                p.dma(p.pool, xch[:], xv[:, :, tok0:tok0 + QT], outs=[xch])
                for f in range(4):
                    ps = psS.get()
                    for k in range(KC):
                        p.op(p.pe, lambda h: h.matmul(ps[:, 0:QT], Wh[:, k, f * 128:(f + 1) * 128], xch[:, k, :],
                                                      start=(k == 0), stop=(k == KC - 1)),
                             ins=[Wh, xch], outs=[ps], signal=(k == KC - 1))
                    if f < 2:
                        p.op(p.act, lambda h: h.activation(qT[:, f, :], ps[:, 0:QT], AF.Copy), ins=[ps], outs=[qT])
                    else:
                        p.op(p.act, lambda h: h.activation(KT[:, f - 2, qt * QT:(qt + 1) * QT], ps[:, 0:QT], AF.Copy),
                             ins=[ps], outs=[KT])
                for blk in range(2):
                    ps = psS.get()
                    for k in range(KC):
                        p.op(p.pe, lambda h: h.matmul(ps[:, 0:256], xch[:, k, blk * 128:(blk + 1) * 128], Wh[:, k, 512:768],
                                                      start=(k == 0), stop=(k == KC - 1)),
                             ins=[Wh, xch], outs=[ps], signal=(k == KC - 1))
                    p.op(p.dve, lambda h: h.tensor_copy(Vc[:, 2 * qt + blk, 0:256], ps[:, 0:256]), ins=[ps], outs=[Vc])
                nkb = 2 * qt + 2
                for kb in range(nkb):
                    rel = kb - 2 * qt
                    for m in range(2):
                        ps = psS.get()
                        p.op(p.pe, lambda h: h.matmul(ps[:, 0:QT], KT[:, m, kb * 128:(kb + 1) * 128], qT[:, m, :], start=True, stop=True),
                             ins=[KT, qT], outs=[ps])
                        pt = pts.get()
                        if rel < -1:
                            p.op(p.act, lambda h: h.activation(pt[:], ps[:, 0:QT], AF.Exp, bias=b31[:, hh:hh + 1], scale=SCALE),
                                 ins=[ps, b31], outs=[pt])
                        else:
                            t = tmpf.get()
                            off = 128 - rel * 128
                            p.op(p.dve, lambda h: h.scalar_tensor_tensor(t[:, 0:QT], ps[:, 0:QT], SCALE, biasall[hh][:, off:off + QT],
                                                                         ALU.mult, ALU.add), ins=[ps, biasall[hh]], outs=[t])
                            p.op(p.act, lambda h: h.activation(pt[:], t[:, 0:QT], AF.Exp), ins=[t], outs=[pt])
                        for qb in range(2):
                            last = 2 * qt + qb
                            if kb <= last:
                                a = acc[m][qb]
                                p.op(p.pe, lambda h: h.matmul(a[:, 0:257], pt[:, qb * 128:(qb + 1) * 128], Vc[:, kb, :],
                                                              start=(kb == 0), stop=(kb == last)),
                                     ins=[pt, Vc], outs=[a])
                for qb in range(2):
                    a1, a2 = acc[0][qb], acc[1][qb]
                    s = sm.get()
                    p.op(p.dve, lambda h: h.reciprocal(s[:, 0:1], a1[:, 256:257]), ins=[a1], outs=[s])
                    p.op(p.dve, lambda h: h.reciprocal(s[:, 1:2], a2[:, 256:257]), ins=[a2], outs=[s])
                    p.op(p.dve, lambda h: h.tensor_tensor(s[:, 1:2], s[:, 1:2], neglam[:], ALU.mult), ins=[s, neglam], outs=[s])
                    o1 = tmpf.get()
                    p.op(p.dve, lambda h: h.tensor_scalar(o1[:, 0:256], a1[:, 0:256], s[:, 0:1], None, ALU.mult), ins=[a1, s], outs=[o1])
                    p.op(p.dve, lambda h: h.scalar_tensor_tensor(o1[:, 0:256], a2[:, 0:256], s[:, 1:2], o1[:, 0:256], ALU.mult, ALU.add),
                         ins=[a2, s, o1], outs=[o1])
                    sq = tmpf.get()
                    p.op(p.dve, lambda h: h.memset(s[:, 2:3], 0.0), outs=[s])
                    p.op(p.act, lambda h: h.activation(sq[:, 0:256], o1[:, 0:256], AF.Square, accum_out=s[:, 2:3]), ins=[o1], outs=[sq, s])
                    p.op(p.dve, lambda h: h.tensor_scalar(s[:, 2:3], s[:, 2:3], 1.0 / 256, 1e-5, ALU.mult, ALU.add), ins=[s], outs=[s])
                    p.op(p.act, lambda h: h.activation(s[:, 2:3], s[:, 2:3], AF.Sqrt), ins=[s], outs=[s])
                    p.op(p.dve, lambda h: h.reciprocal(s[:, 3:4], s[:, 2:3]), ins=[s], outs=[s])
                    p.op(p.dve, lambda h: h.tensor_scalar(s[:, 3:4], s[:, 3:4], 1.0 - LAM_INIT0, None, ALU.mult), ins=[s], outs=[s])
                    o2 = ob.get()
                    p.op(p.dve, lambda h: h.scalar_tensor_tensor(o2[:], o1[:, 0:256], s[:, 3:4], subw[:], ALU.mult, ALU.mult),
                         ins=[o1, s, subw], outs=[o2])
                    r0 = tok0 + qb * 128
                    p.dma(p.sp, d["o"][hh, r0:r0 + 128, :], o2[:], ins=[o2])
    return [ob.t[0], ob.t[1]]


def build_att(seq=SEQ, nbatch=2):
    nc = bass.Bass("TRN2", target_bir_lowering=False)
    d = declare_att(nc, seq * nbatch)
    p = P(nc)
    outs = emit_att(p, d, seq, nbatch)
    p.finish(outs)
    print("att stats", p.stats())
    return nc


import math

SEQ = 8192
HC = 512
GN_EPS = 64e-5
TC = 64
R1_OUT = ["R", "Wd", "K", "V", "A", "B", "G"]


def declare_r1(nc, seq, nbatch):
    d = {}
    ntok = seq * nbatch

    def inp(name, shape, dt=F32):
        d[name] = nc.dram_tensor(name, list(shape), dt, kind="ExternalInput").ap()

    inp("x1T", [128, KC, nbatch * (seq + 1)])
    inp("mix", [128, 6, KC])
    for n in ("wr", "wk", "wv"):
        inp(n, [128, KC, HC])
    inp("w1", [128, KC, 128])
    inp("a1", [128, KC, 128])
    inp("g1", [128, KC, 480])
    inp("w2s", [128, HC])
    inp("a2s", [128, HC])
    inp("g2s", [120, 4, HC])
    for n in ("w0", "a0", "kkv", "kav", "rkv"):
        inp(n, [128, HC])
    for n in R1_OUT:
        d[n] = nc.dram_tensor(n, [ntok, HC], F32, kind="ExternalOutput").ap()
    d["RKR"] = nc.dram_tensor("RKR", [ntok, 8], F32, kind="ExternalOutput").ap()
    return d


TB = 64


def emit_r1(p, d, seq, nbatch):
    Wr, Wk, Wv = [p.sb(n, [128, KC, HC], BF16) for n in ("Wr", "Wk", "Wv")]
    W1 = p.sb("W1", [128, KC, 128], BF16)
    A1 = p.sb("A1", [128, KC, 128], BF16)
    G1 = p.sb("G1", [128, KC, 480], BF16)
    W2 = p.sb("W2", [128, HC], BF16)
    A2 = p.sb("A2", [128, HC], BF16)
    G2 = p.sb("G2", [120, 4, HC], BF16)
    mix = p.sb("mix", [128, 6, KC], F32)
    omm = p.sb("omm", [128, 6, KC], F32)
    vec = {n: p.sb("v_" + n, [128, HC], F32) for n in ("w0", "a0", "kkv", "kav", "rkv")}
    xf = p.sb("xf", [128, KC, TB + 1], F32)
    xmr = Rot([p.sb(f"xm{i}", [128, KC, TB], BF16) for i in range(2)])
    hT = Rot([p.sb(f"hT{i}", [128, TB], BF16) for i in range(3)])
    hgT = p.sb("hgT", [120, 4, TB], BF16)
    psA = Rot([p.ps(f"psA{i}", [128, 512]) for i in range(6)])
    psB = Rot([p.ps(f"psB{i}", [128, 512]) for i in range(2)])
    tf = Rot([p.sb(f"tf{i}", [TB, HC], F32) for i in range(14)])
    sm = Rot([p.sb(f"sm{i}", [TB, 8], F32) for i in range(6)])

    for (t, n) in ((Wr, "wr"), (Wk, "wk"), (Wv, "wv"), (W1, "w1"), (A1, "a1"), (G1, "g1"), (W2, "w2s"), (A2, "a2s"), (G2, "g2s")):
        p.dma(p.pool, t[:], d[n], outs=[t])
    p.dma(p.sp, mix[:], d["mix"], outs=[mix])
    p.op(p.dve, lambda h: h.tensor_scalar(omm[:], mix[:], -1.0, 1.0, ALU.mult, ALU.add), ins=[mix], outs=[omm])
    for n, t in vec.items():
        p.dma(p.sp, t[:], d[n], outs=[t])
    V = {n: t[0:TB, :] for n, t in vec.items()}

    nblk = seq // TB
    for b in range(nbatch):
        for blk in range(nblk):
            c0 = b * (seq + 1) + blk * TB
            r0 = b * seq + blk * TB
            p.dma(p.sp, xf[:], d["x1T"][:, :, c0:c0 + TB + 1], outs=[xf])
            cnt = [0]

            def mixed(i):
                xm = xmr.get()
                mb = mix[:, i, :].rearrange("p (k o) -> p k o", o=1).broadcast_to([128, KC, TB])
                ob = omm[:, i, :].rearrange("p (k o) -> p k o", o=1).broadcast_to([128, KC, TB])
                cnt[0] += 1
                eng = p.dve if cnt[0] % 2 == 0 else p.pool
                tmpx = xmr.get()
                p.op(eng, lambda h: h.tensor_tensor(tmpx[:], xf[:, :, 0:TB], mb, ALU.mult), ins=[xf, mix], outs=[tmpx])
                p.op(eng, lambda h: h.tensor_tensor(xm[:], xf[:, :, 1:TB + 1], ob, ALU.mult), ins=[xf, omm], outs=[xm])
                p.op(eng, lambda h: h.tensor_tensor(xm[:], xm[:], tmpx[:], ALU.add), ins=[xm, tmpx], outs=[xm])
                return xm

            def proj(i, Wt):
                xin = mixed(i)
                ps = psA.get()
                for k in range(KC):
                    p.op(p.pe, lambda h: h.matmul(ps[0:TB, 0:HC], xin[:, k, :], Wt[:, k, :], start=(k == 0), stop=(k == KC - 1)),
                         ins=[xin, Wt], outs=[ps], signal=(k == KC - 1))
                return ps

            def lora1(xin, Wt, c0_, ncol, func, dst, dst_ap):
                ps = psB.get()
                for k in range(KC):
                    p.op(p.pe, lambda h: h.matmul(ps[0:ncol, 0:TB], Wt[:, k, c0_:c0_ + ncol], xin[:, k, :], start=(k == 0), stop=(k == KC - 1)),
                         ins=[xin, Wt], outs=[ps], signal=(k == KC - 1))
                p.op(p.act, lambda h: h.activation(dst_ap, ps[0:ncol, 0:TB], func), ins=[ps], outs=[dst])

            ps_r = proj(0, Wr)
            rs = tf.get()
            p.op(p.act, lambda h: h.activation(rs[:], ps_r[0:TB, 0:HC], AF.Copy), ins=[ps_r], outs=[rs])
            ps_v = proj(3, Wv)
            vs = tf.get()
            p.op(p.act, lambda h: h.activation(vs[:], ps_v[0:TB, 0:HC], AF.Copy), ins=[ps_v], outs=[vs])
            ps_k = proj(2, Wk)
            ks = tf.get()
            p.op(p.act, lambda h: h.activation(ks[:], ps_k[0:TB, 0:HC], AF.Copy), ins=[ps_k], outs=[ks])
            hw = hT.get()
            lora1(mixed(1), W1, 0, 128, AF.Tanh, hw, hw[:])
            ps_w = psA.get()
            p.op(p.pe, lambda h: h.matmul(ps_w[0:TB, 0:HC], hw[:], W2[:], start=True, stop=True), ins=[hw, W2], outs=[ps_w])
            zt = tf.get()
            p.op(p.dve, lambda h: h.tensor_tensor(zt[:], ps_w[0:TB, 0:HC], V["w0"], ALU.add), ins=[ps_w, vec["w0"]], outs=[zt])
            p.op(p.act, lambda h: h.activation(zt[:], zt[:], AF.Sigmoid), ins=[zt], outs=[zt])
            wd = tf.get()
            p.op(p.act, lambda h: h.activation(wd[:], zt[:], AF.Exp, scale=-math.exp(-0.5)), ins=[zt], outs=[wd])
            ha = hT.get()
            lora1(mixed(4), A1, 0, 128, AF.Copy, ha, ha[:])
            ps_a = psA.get()
            p.op(p.pe, lambda h: h.matmul(ps_a[0:TB, 0:HC], ha[:], A2[:], start=True, stop=True), ins=[ha, A2], outs=[ps_a])
            at = tf.get()
            p.op(p.dve, lambda h: h.tensor_tensor(at[:], ps_a[0:TB, 0:HC], V["a0"], ALU.add), ins=[ps_a, vec["a0"]], outs=[at])
            p.op(p.act, lambda h: h.activation(at[:], at[:], AF.Sigmoid), ins=[at], outs=[at])
            xg = mixed(5)
            for gi in range(4):
                lora1(xg, G1, gi * 120, 120, AF.Sigmoid, hgT, hgT[:, gi, :])
            ps_g = psA.get()
            for gi in range(4):
                p.op(p.pe, lambda h: h.matmul(ps_g[0:TB, 0:HC], hgT[:, gi, :], G2[:, gi, :], start=(gi == 0), stop=(gi == 3)),
                     ins=[hgT, G2], outs=[ps_g], signal=(gi == 3))
            gs = tf.get()
            p.op(p.act, lambda h: h.activation(gs[:], ps_g[0:TB, 0:HC], AF.Copy), ins=[ps_g], outs=[gs])
            v3 = lambda t: t[:].rearrange("p (a b) -> p a b", a=8)
            kk = tf.get()
            p.op(p.dve, lambda h: h.tensor_tensor(kk[:], ks[:], V["kkv"], ALU.mult), ins=[ks, vec["kkv"]], outs=[kk])
            sq = tf.get()
            p.op(p.act, lambda h: h.activation(sq[:], kk[:], AF.Square), ins=[kk], outs=[sq])
            s1 = sm.get()
            p.op(p.dve, lambda h: h.tensor_reduce(s1[:], v3(sq), AX.X, ALU.add), ins=[sq], outs=[s1])
            p.op(p.act, lambda h: h.activation(s1[:], s1[:], AF.Sqrt), ins=[s1], outs=[s1])
            p.op(p.dve, lambda h: h.tensor_scalar(s1[:], s1[:], 1e-12, None, ALU.max), ins=[s1], outs=[s1])
            p.op(p.dve, lambda h: h.reciprocal(s1[:], s1[:]), ins=[s1], outs=[s1])
            s1b = s1[:].rearrange("p (a o) -> p a o", o=1).broadcast_to([TB, 8, 64])
            p.op(p.dve, lambda h: h.tensor_tensor(v3(kk), v3(kk), s1b, ALU.mult), ins=[kk, s1], outs=[kk])
            An = tf.get()
            p.op(p.pool, lambda h: h.tensor_scalar(An[:], kk[:], -1.0, None, ALU.mult), ins=[kk], outs=[An])
            Bn = tf.get()
            p.op(p.pool, lambda h: h.tensor_tensor(Bn[:], kk[:], at[:], ALU.mult), ins=[kk, at], outs=[Bn])
            t1 = tf.get()
            p.op(p.dve, lambda h: h.scalar_tensor_tensor(t1[:], at[:], -1.0, V["kav"], ALU.add, ALU.mult), ins=[at, vec["kav"]], outs=[t1])
            km = tf.get()
            p.op(p.dve, lambda h: h.scalar_tensor_tensor(km[:], t1[:], 1.0, ks[:], ALU.add, ALU.mult), ins=[t1, ks], outs=[km])
            t2 = tf.get()
            p.op(p.pool, lambda h: h.tensor_tensor(t2[:], rs[:], km[:], ALU.mult), ins=[rs, km], outs=[t2])
            p.op(p.pool, lambda h: h.tensor_tensor(t2[:], t2[:], V["rkv"], ALU.mult), ins=[t2, vec["rkv"]], outs=[t2])
            s2 = sm.get()
            p.op(p.dve, lambda h: h.tensor_reduce(s2[:], v3(t2), AX.X, ALU.add), ins=[t2], outs=[s2])
            for (n, t) in (("R", rs), ("Wd", wd), ("K", km), ("V", vs), ("A", An), ("B", Bn), ("G", gs)):
                p.dma(p.sp, d[n][r0:r0 + TB, :], t[:], ins=[t])
            p.dma(p.sp, d["RKR"][r0:r0 + TB, :], s2[:], ins=[s2])
    return tf.t + sm.t


def build_r1(seq=SEQ, nbatch=2):
    nc = bass.Bass("TRN2", target_bir_lowering=False)
    d = declare_r1(nc, seq, nbatch)
    p = P(nc)
    outs = emit_r1(p, d, seq, nbatch)
    p.finish(outs)
    print("r1 stats", p.stats())
    return nc


def declare_r2(nc, seq, nbatch):
    d = {}
    ntok = seq * nbatch

    def inp(name, shape, dt=F32):
        d[name] = nc.dram_tensor(name, list(shape), dt, kind="ExternalInput").ap()

    for n in R1_OUT:
        inp(n, [ntok, HC])
    inp("RKR", [ntok, 8])
    inp("lnw", [128, HC])
    inp("lnb", [128, HC])
    d["Y"] = nc.dram_tensor("Y", [ntok, HC], F32, kind="Internal").ap()
    d["o"] = nc.dram_tensor("o", [ntok, HC], BF16, kind="ExternalOutput").ap()
    return d


def emit_r2(p, d, seq, nbatch):
    assert nbatch == 2
    S = p.sb("S", [128, 8, 64], F32)
    names = ["A", "Wd", "B", "K", "R"]
    bufs = {n: [p.sb(f"{n}{i}", [128, TC, 64], F32) for i in range(2)] for n in names}
    vbuf = [p.sb(f"Vb{i}", [128, TC, 8], F32) for i in range(2)]
    ybuf = [p.sb(f"Yb{i}", [128, TC, 8], F32) for i in range(2)]
    tmp = Rot([p.sb(f"tmp{i}", [128, 8, 64], F32) for i in range(3)])
    t2r = Rot([p.sb(f"t2_{i}", [128, 8, 64], F32) for i in range(2)])
    t3r = Rot([p.sb(f"t3_{i}", [128, 8, 64], F32) for i in range(3)])
    sar = Rot([p.sb(f"sa{i}", [128, 8], F32) for i in range(3)])
    Ydram = p.view("Ydram", d["Y"])
    p.op(p.dve, lambda h: h.memset(S[:], 0.0), outs=[S])
    nchunk = seq // TC
    qs = [p.sp, p.act]
    qi = [0]

    def q():
        qi[0] += 1
        return qs[qi[0] % 2]

    def load(ci):
        t0 = ci * TC
        s = ci % 2
        for n in names:
            for pr in range(16):
                b, hl = pr // 8, pr % 8
                src = d[n][b * seq + t0:b * seq + t0 + TC, hl * 64:(hl + 1) * 64]
                p.dma(q(), bufs[n][s][pr * 8:(pr + 1) * 8, :, :],
                      src.rearrange("(o t) k -> o t k", o=1).broadcast_to([8, TC, 64]), outs=[bufs[n][s]])
        for pr in range(16):
            b, hl = pr // 8, pr % 8
            src = d["V"][b * seq + t0:b * seq + t0 + TC, hl * 64:(hl + 1) * 64]
            p.dma(q(), vbuf[s][pr * 8:(pr + 1) * 8, :, :], src.rearrange("t (vb vi) -> vb t vi", vb=8), outs=[vbuf[s]])

    def bc(ap2):
        return ap2.rearrange("p (o k) -> p o k", o=1).broadcast_to([128, 8, 64])

    def bk(ap2):
        return ap2.rearrange("p (v o) -> p v o", o=1).broadcast_to([128, 8, 64])

    load(0)
    for ci in range(nchunk):
        s = ci % 2
        if ci + 1 < nchunk:
            load(ci + 1)
        Ab, Wb, Bb, Kb, Rb = (bufs[n][s] for n in names)
        Vb, Yb = vbuf[s], ybuf[s]
        for i in range(TC):
            t3 = t3r.get()
            p.op(p.pool, lambda h: h.tensor_tensor(t3[:], bk(Vb[:, i, :]), bc(Kb[:, i, :]), ALU.mult), ins=[Vb, Kb], outs=[t3])
            t1 = tmp.get()
            p.op(p.dve, lambda h: h.tensor_tensor(t1[:], S[:], bc(Ab[:, i, :]), ALU.mult), ins=[S, Ab], outs=[t1])
            sa = sar.get()
            p.op(p.dve, lambda h: h.tensor_reduce(sa[:], t1[:], AX.X, ALU.add), ins=[t1], outs=[sa])
            p.op(p.pool, lambda h: h.tensor_tensor(S[:], S[:], bc(Wb[:, i, :]), ALU.mult), ins=[S, Wb], outs=[S])
            t2 = t2r.get()
            p.op(p.dve, lambda h: h.tensor_tensor(t2[:], bk(sa[:]), bc(Bb[:, i, :]), ALU.mult), ins=[sa, Bb], outs=[t2])
            p.op(p.pool, lambda h: h.tensor_tensor(S[:], S[:], t3[:], ALU.add), ins=[S, t3], outs=[S])
            p.op(p.dve, lambda h: h.tensor_tensor(S[:], S[:], t2[:], ALU.add), ins=[S, t2], outs=[S])
            t4 = tmp.get()
            p.op(p.pool, lambda h: h.tensor_tensor(t4[:], S[:], bc(Rb[:, i, :]), ALU.mult), ins=[S, Rb], outs=[t4])
            p.op(p.dve, lambda h: h.tensor_reduce(Yb[:, i, :], t4[:], AX.X, ALU.add), ins=[t4], outs=[Yb])
        t0 = ci * TC
        for pr in range(16):
            b, hl = pr // 8, pr % 8
            dst = d["Y"][b * seq + t0:b * seq + t0 + TC, hl * 64:(hl + 1) * 64]
            p.dma(q(), dst.rearrange("t (vb vi) -> vb t vi", vb=8), Yb[pr * 8:(pr + 1) * 8, :, :], ins=[Yb], outs=[Ydram], owner=Yb)

    p._wait(p.sp, p._need([], ybuf))
    lnw = p.sb("lnw", [128, HC], F32)
    lnb = p.sb("lnb", [128, HC], F32)
    p.dma(p.sp, lnw[:], d["lnw"], outs=[lnw])
    p.dma(p.sp, lnb[:], d["lnb"], outs=[lnb])
    tf = Rot([p.sb(f"pf{i}", [128, HC], F32) for i in range(5)])
    sm = Rot([p.sb(f"ps{i}", [128, 8], F32) for i in range(8)])
    obr = Rot([p.sb(f"ob{i}", [128, HC], BF16) for i in range(2)])
    v3 = lambda t: t[:].rearrange("p (a b) -> p a b", a=8)
    b3 = lambda t: t[:].rearrange("p (a o) -> p a o", o=1).broadcast_to([128, 8, 64])
    ntok = seq * nbatch
    for blk in range(ntok // 128):
        r0 = blk * 128
        y = tf.get()
        p.dma(p.sp, y[:], d["Y"][r0:r0 + 128, :], ins=[Ydram], outs=[y])
        vv = tf.get()
        p.dma(p.sp, vv[:], d["V"][r0:r0 + 128, :], outs=[vv])
        gg = tf.get()
        p.dma(p.sp, gg[:], d["G"][r0:r0 + 128, :], outs=[gg])
        rk = sm.get()
        p.dma(p.sp, rk[:], d["RKR"][r0:r0 + 128, :], outs=[rk])
        mu = sm.get()
        p.op(p.dve, lambda h: h.tensor_reduce(mu[:], v3(y), AX.X, ALU.add), ins=[y], outs=[mu])
        p.op(p.dve, lambda h: h.tensor_scalar(mu[:], mu[:], 1.0 / 64, None, ALU.mult), ins=[mu], outs=[mu])
        p.op(p.dve, lambda h: h.tensor_tensor(v3(y), v3(y), b3(mu), ALU.subtract), ins=[y, mu], outs=[y])
        sq = tf.get()
        p.op(p.act, lambda h: h.activation(sq[:], y[:], AF.Square), ins=[y], outs=[sq])
        var = sm.get()
        p.op(p.dve, lambda h: h.tensor_reduce(var[:], v3(sq), AX.X, ALU.add), ins=[sq], outs=[var])
        p.op(p.dve, lambda h: h.tensor_scalar(var[:], var[:], 1.0 / 64, GN_EPS, ALU.mult, ALU.add), ins=[var], outs=[var])
        p.op(p.act, lambda h: h.activation(var[:], var[:], AF.Sqrt), ins=[var], outs=[var])
        p.op(p.dve, lambda h: h.reciprocal(var[:], var[:]), ins=[var], outs=[var])
        p.op(p.dve, lambda h: h.tensor_tensor(v3(y), v3(y), b3(var), ALU.mult), ins=[y, var], outs=[y])
        p.op(p.pool, lambda h: h.tensor_tensor(y[:], y[:], lnw[:], ALU.mult), ins=[y, lnw], outs=[y])
        p.op(p.pool, lambda h: h.tensor_tensor(y[:], y[:], lnb[:], ALU.add), ins=[y, lnb], outs=[y])
        p.op(p.dve, lambda h: h.tensor_tensor(v3(vv), v3(vv), b3(rk), ALU.mult), ins=[vv, rk], outs=[vv])
        p.op(p.dve, lambda h: h.tensor_tensor(y[:], y[:], vv[:], ALU.add), ins=[y, vv], outs=[y])
        ob = obr.get()
        p.op(p.dve, lambda h: h.tensor_tensor(ob[:], y[:], gg[:], ALU.mult), ins=[y, gg], outs=[ob])
        p.dma(p.sp, d["o"][r0:r0 + 128, :], ob[:], ins=[ob])
    return obr.t


def build_r2(seq=SEQ, nbatch=2):
    nc = bass.Bass("TRN2", target_bir_lowering=False)
    d = declare_r2(nc, seq, nbatch)
    p = P(nc)
    outs = emit_r2(p, d, seq, nbatch)
    p.finish(outs)
    print("r2 stats", p.stats())
    return nc


import ml_dtypes
from concourse.bass_utils import run_bass_kernel_spmd

NCORE = 8


def fm(a):
    return np.ascontiguousarray(a.T.reshape(-1, 128, a.shape[0]).transpose(1, 0, 2))


def panels(w):
    K, M = w.shape
    return np.ascontiguousarray(w.reshape(K // 128, 128, M // 128, 128).transpose(2, 1, 0, 3))


def tile_k(w):
    K, M = w.shape
    return np.ascontiguousarray(w.reshape(K // 128, 128, M).transpose(1, 0, 2))


def rep(v):
    return np.ascontiguousarray(np.broadcast_to(v[None], (128, v.shape[0]))).astype(np.float32)


def prep_pm_weights(I, i, mixer_wo):
    F = 11008
    w = {}
    w["wo"] = panels(mixer_wo)
    w["caq"] = panels(I["ca_w_q"][i])
    kv = I["ca_w_kv"][i]
    w["cak"] = panels(kv[:, :512])
    w["cav"] = panels(kv[:, 512:])
    w["cao"] = panels(I["ca_w_o"][i])
    up = I["ffn_w_up"][i]
    w["wupg"] = panels(up[:, :F])
    w["wupv"] = panels(up[:, F:])
    w["wdn"] = panels(I["ffn_w_down"][i])
    cw = I["ffn_conv_w"][i]
    w["convw"] = np.ascontiguousarray(cw.reshape(3, 2, FC, 128).transpose(3, 2, 1, 0))
    w["convb"] = np.ascontiguousarray(I["ffn_conv_b"][i].reshape(2, FC, 128).transpose(2, 1, 0))
    w["lng"] = np.ascontiguousarray(I["ln_g"][i].reshape(3, KC, 128).transpose(2, 0, 1))
    w["lnb"] = np.ascontiguousarray(I["ln_b"][i].reshape(3, KC, 128).transpose(2, 0, 1))
    return w


def run_pm(I, layer, mixer_wo, mix_bf, xflat):
    w = prep_pm_weights(I, layer, mixer_wo)
    mem = I["mem"]
    maps = []
    for c in range(NCORE):
        b, t0 = c // 4, (c % 4) * 2048
        g0 = b * SEQ + t0
        if t0 == 0:
            mrows = np.concatenate([np.zeros((2, D), mix_bf.dtype), mix_bf[g0:g0 + 2048]], 0)
            xrows = np.concatenate([np.zeros((2, D), np.float32), xflat[g0:g0 + 2048]], 0)
        else:
            mrows = mix_bf[g0 - 2:g0 + 2048]
            xrows = xflat[g0 - 2:g0 + 2048]
        m = dict(w)
        m["mixT"] = fm(mrows)
        m["xT"] = fm(xrows)
        m["memT"] = fm(mem[b])
        m["hmask"] = np.full((128, 1), 0.0 if t0 == 0 else 1.0, np.float32)
        maps.append(m)
    nc = build_pm(NT)
    res = run_bass_kernel_spmd(nc, maps, core_ids=list(range(NCORE)))
    out = np.empty((2 * SEQ, D), np.float32)
    for c in range(NCORE):
        b, t0 = c // 4, (c % 4) * 2048
        g0 = b * SEQ + t0
        yT = np.asarray(res.results[c]["yT"])
        out[g0:g0 + 2048] = yT.transpose(2, 1, 0).reshape(NCOL, D)[2:]
    return out


def att_inputs(I, xT, c):
    wq = I["da_w_qkv"][0]
    m = {"xT": xT}
    ws = []
    for hh in range(2):
        h = 2 * c + hh
        cols = np.concatenate([np.arange(h * 256, (h + 1) * 256), 4096 + np.arange(h * 256, (h + 1) * 256),
                               8192 + np.arange(h * 256, (h + 1) * 256)])
        ws.append(wq[:, cols].reshape(KC, 128, 768).transpose(1, 0, 2))
    m["wqkv"] = np.ascontiguousarray(np.stack(ws))
    rb = I["rel_bias"][:, 2 * c:2 * c + 2]
    m["relb"] = np.ascontiguousarray(np.broadcast_to(rb[None], (128, 32, 2)))
    m["lam"] = np.ascontiguousarray(np.broadcast_to(I["da_lam"][0][None], (128, 4, 128)))
    m["subw"] = np.ascontiguousarray(np.broadcast_to(I["da_subln"][0][None], (128, 256)))
    k = np.arange(128)[:, None]
    j = np.arange(512)[None, :]
    m["nidx"] = (j - k - 128).astype(np.float32)
    return m


def r1_inputs(I, x1, c):
    B, S, _ = x1.shape
    cols = slice(c * 512, (c + 1) * 512)
    m = {}
    m["mix"] = np.ascontiguousarray(I["rw_mix"][0].reshape(6, KC, 128).transpose(2, 0, 1))
    rkv = I["rw_w_rkv"][0]
    m["wr"] = tile_k(rkv[0][:, cols])
    m["wk"] = tile_k(rkv[1][:, cols])
    m["wv"] = tile_k(rkv[2][:, cols])
    m["w2s"] = np.ascontiguousarray(I["rw_w2"][0][:, cols])
    m["a2s"] = np.ascontiguousarray(I["rw_a2"][0][:, cols])
    m["g2s"] = np.ascontiguousarray(I["rw_g2"][0][:, cols].reshape(4, 120, 512).transpose(1, 0, 2))
    m["w0"] = rep(I["rw_w0"][0][cols])
    m["a0"] = rep(I["rw_a0"][0][cols])
    m["kkv"] = rep(I["rw_k_k"][0][cols])
    m["kav"] = rep(I["rw_k_a"][0][cols])
    m["rkv"] = rep(I["rw_r_k"][0].reshape(-1)[cols])
    return m


def kernel(**I):
    I = {k: np.asarray(v) for k, v in I.items()}
    x = I["x"].astype(np.float32, copy=False)
    xflat = x.reshape(2 * SEQ, D)
    xT = fm(xflat)
    nc = build_att(SEQ, 2)
    res = run_bass_kernel_spmd(nc, [att_inputs(I, xT, c) for c in range(NCORE)], core_ids=list(range(NCORE)))
    mix0 = np.empty((2 * SEQ, D), ml_dtypes.bfloat16)
    for c in range(NCORE):
        o = np.asarray(res.results[c]["o"])
        for hh in range(2):
            h = 2 * c + hh
            mix0[:, h * 256:(h + 1) * 256] = o[hh]
    del xT, res
    x1 = run_pm(I, 0, I["da_w_o"][0], mix0, xflat)
    x1b = x1.reshape(2, SEQ, D)
    parts = []
    for b in range(2):
        xp = np.concatenate([np.zeros((1, D), np.float32), x1b[b]], 0)
        parts.append(xp.T.reshape(KC, 128, SEQ + 1).transpose(1, 0, 2))
    x1T = np.ascontiguousarray(np.concatenate(parts, 2))
    shared = {"x1T": x1T, "w1": tile_k(I["rw_w1"][0]), "a1": tile_k(I["rw_a1"][0]), "g1": tile_k(I["rw_g1"][0])}
    maps = []
    for c in range(NCORE):
        m = r1_inputs(I, x1b, c)
        m.update(shared)
        maps.append(m)
    nc = build_r1(SEQ, 2)
    res1 = run_bass_kernel_spmd(nc, maps, core_ids=list(range(NCORE)))
    del maps, x1T, shared
    maps = []
    for c in range(NCORE):
        cols = slice(c * 512, (c + 1) * 512)
        m = {n: np.asarray(res1.results[c][n]) for n in R1_OUT + ["RKR"]}
        m["lnw"] = rep(I["rw_lnx_w"][0][cols])
        m["lnb"] = rep(I["rw_lnx_b"][0][cols])
        maps.append(m)
    nc = build_r2(SEQ, 2)
    res2 = run_bass_kernel_spmd(nc, maps, core_ids=list(range(NCORE)))
    mix1 = np.empty((2 * SEQ, D), ml_dtypes.bfloat16)
    for c in range(NCORE):
        mix1[:, c * 512:(c + 1) * 512] = np.asarray(res2.results[c]["o"])
    del maps, res1, res2
    out = run_pm(I, 1, I["rw_w_o"][0], mix1, x1)
    return out.reshape(2, SEQ, D)
```

```python
import numpy as np
import concourse.bass as bass
import concourse.mybir as mybir

F32 = mybir.dt.float32
BF16 = mybir.dt.bfloat16
I32 = mybir.dt.int32
AF = mybir.ActivationFunctionType
ALU = mybir.AluOpType
AX = mybir.AxisListType


class T:
    __slots__ = ("name", "h", "writer", "readers", "dsem", "dcnt", "p")

    def __init__(self, p, name, h):
        self.p = p
        self.name = name
        self.h = h
        self.writer = None
        self.readers = {}
        self.dsem = None
        self.dcnt = 0

    def __getitem__(self, k):
        return self.h[k]

    def sem(self):
        if self.dsem is None:
            self.dsem = self.p.new_sem("d_" + self.name)
        return self.dsem


class Eng:
    def __init__(self, p, name, h, selfsync):
        self.p = p
        self.name = name
        self.h = h
        self.sem = p.new_sem("e_" + name)
        self.count = 0
        self.known = {}
        self.pending = False
        self.selfsync = selfsync
        self.ninst = 0


class P:
    def __init__(self, nc):
        self.nc = nc
        self.nsem = 0
        self.pe = Eng(self, "pe", nc.tensor, False)
        self.act = Eng(self, "act", nc.scalar, True)
        self.dve = Eng(self, "dve", nc.vector, True)
        self.pool = Eng(self, "pool", nc.gpsimd, True)
        self.sp = Eng(self, "sp", nc.sync, False)
        self.engs = [self.pe, self.act, self.dve, self.pool, self.sp]
        self.ntile = 0
        self.guards = []
        self.tiles = []
        self.phase = 0
        self.bsem = None
        self.bcnt = 0

    def new_sem(self, name):
        self.nsem += 1
        return self.nc.alloc_semaphore(f"{name}_{self.nsem}")

    def sb(self, name, shape, dt):
        g = self.nc.sbuf_tensor(f"s{self.phase}_" + name, list(shape), dt)
        t = T(self, name, g.__enter__())
        self.guards.append(g)
        self.tiles.append(t)
        return t

    def ps(self, name, shape, dt=F32):
        g = self.nc.psum_tensor(f"p{self.phase}_" + name, list(shape), dt)
        t = T(self, name, g.__enter__())
        self.guards.append(g)
        self.tiles.append(t)
        return t

    def view(self, name, ap):
        t = T(self, name, ap)
        self.tiles.append(t)
        return t

    def barrier_and_free(self):
        for e in self.engs:
            assert not e.pending, e.name
        sp = self.sp
        for e in self.engs:
            if e is not sp and e.count:
                sp.h.wait_ge(e.sem, e.count)
        for t in self.tiles:
            if t.dsem is not None and t.dcnt:
                sp.h.wait_ge(t.dsem, t.dcnt)
        if self.bsem is None:
            self.bsem = self.new_sem("barrier")
        self.bcnt += 1
        sp.h.sem_inc(self.bsem, 1)
        for e in self.engs:
            if e is not sp:
                e.h.wait_ge(self.bsem, self.bcnt)
        for g in reversed(self.guards):
            g.__exit__(None, None, None)
        self.guards = []
        self.tiles = []
        self.phase += 1

    def _need(self, ins, outs):
        need = {}

        def add(ev):
            if ev is None:
                return
            s, v = ev
            k = id(s)
            if k not in need or need[k][1] < v:
                need[k] = (s, v)

        for t in ins:
            add(t.writer)
        for t in outs:
            add(t.writer)
            for ev in t.readers.values():
                add(ev)
        return need

    def _wait(self, e, need):
        for k, (s, v) in need.items():
            if s is e.sem and not e.selfsync:
                continue
            if e.known.get(k, 0) >= v:
                continue
            e.h.wait_ge(s, v)
            e.ninst += 1
            e.known[k] = v

    def _mark(self, ins, outs, ev):
        k = id(ev[0])
        for t in ins:
            old = t.readers.get(k)
            if old is None or old[1] < ev[1]:
                t.readers[k] = ev
        for t in outs:
            t.writer = ev
            t.readers = {}

    def op(self, e, fn, ins=(), outs=(), signal=True):
        need = self._need(ins, outs)
        self._wait(e, need)
        inst = fn(e.h)
        e.ninst += 1
        if signal:
            e.count += 1
            inst.then_inc(e.sem, 1)
            e.pending = False
            ev = (e.sem, e.count)
        else:
            e.pending = True
            ev = (e.sem, e.count + 1)
        self._mark(ins, outs, ev)
        return ev

    def dma(self, q, out_ap, in_ap, ins=(), outs=(), owner=None):
        need = self._need(ins, outs)
        self._wait(q, need)
        if owner is None:
            owner = outs[0] if outs else ins[0]
        s = owner.sem()
        inst = q.h.dma_start(out=out_ap, in_=in_ap)
        q.ninst += 1
        owner.dcnt += 16
        inst.then_inc(s, 16)
        ev = (s, owner.dcnt)
        self._mark(ins, outs, ev)
        return ev

    def finish(self, tiles):
        for e in self.engs:
            assert not e.pending, e.name
        need = self._need(tiles, tiles)
        self._wait(self.sp, need)
        for e in self.engs:
            if e.count and e is not self.sp:
                self.sp.h.wait_ge(e.sem, e.count)

    def stats(self):
        return {e.name: e.ninst for e in self.engs} | {"nsem": self.nsem}


D = 4096
KC = 32
TT = 410
NT = 5
NCOL = NT * TT
FC = 86
FPARTS = [(0, 22), (22, 44), (44, 65), (65, 86)]
NMEM = 256
ALPHA = float(4 ** 0.25)
LN_EPS = 1e-5
NB = 5


class Dummy:
    def __getitem__(self, k):
        return self

    def rearrange(self, *a, **k):
        return self


class DryP:
    dry = True

    def __init__(self):
        self.pe = self.act = self.dve = self.pool = self.sp = None

    def sb(self, *a):
        return Dummy()

    ps = sb
    view = sb

    def op(self, *a, **k):
        pass

    dma = op


class WStream:
    def __init__(self, p, nb, seq=None):
        self.p = p
        self.nb = nb
        self.record = seq is None
        self.seq = [] if seq is None else seq
        self.cur = 0
        self.loaded = 0
        self.bufs = [p.sb(f"wb{i}", [128, 32 * 128], BF16) for i in range(nb)]

    def next(self, ap, n):
        if self.record:
            self.seq.append((ap, n))
            return self.bufs[0]
        p = self.p
        i = self.cur
        self.cur += 1
        while self.loaded < min(len(self.seq), i + self.nb):
            j = self.loaded
            apj, nj = self.seq[j]
            b = self.bufs[j % self.nb]
            p.dma(p.pool, b[:, 0:nj * 128], apj.rearrange("p n j -> p (n j)"), outs=[b])
            self.loaded += 1
        return self.bufs[i % self.nb]


class Rot:
    def __init__(self, tiles):
        self.t = tiles
        self.i = 0

    def get(self):
        t = self.t[self.i % len(self.t)]
        self.i += 1
        return t


def declare_pm(nc):
    d = {}

    def inp(name, shape, dt=F32):
        d[name] = nc.dram_tensor(name, list(shape), dt, kind="ExternalInput").ap()

    inp("mixT", [128, KC, NCOL], BF16)
    inp("xT", [128, KC, NCOL])
    inp("hmask", [128, 1])
    inp("wo", [32, 128, 32, 128])
    inp("caq", [4, 128, 32, 128])
    inp("cak", [4, 128, 32, 128])
    inp("cav", [4, 128, 32, 128])
    inp("cao", [32, 128, 4, 128])
    inp("memT", [128, KC, NMEM])
    inp("wupg", [FC, 128, 32, 128])
    inp("wupv", [FC, 128, 32, 128])
    inp("wdn", [32, 128, FC, 128])
    inp("convw", [128, FC, 2, 3])
    inp("convb", [128, FC, 2])
    inp("lng", [128, 3, KC])
    inp("lnb", [128, 3, KC])
    d["yT"] = nc.dram_tensor("yT", [128, KC, NCOL], F32, kind="ExternalOutput").ap()
    return d


def emit_pm(p, d, ws, ntiles=NT):
    W = TT
    xres_all = p.sb("xres", [128, KC, W], F32)
    xbf_all = p.sb("xbf", [128, KC, W], BF16)
    xres = [p.view(f"xres{k}", xres_all[:, k, :]) for k in range(KC)]
    xbf = [p.view(f"xbf{k}", xbf_all[:, k, :]) for k in range(KC)]
    gpart = p.sb("gpart", [128, 22 * W], BF16)
    kT = p.sb("kT", [128, 4, NMEM], BF16)
    vm = p.sb("vm", [128, 2, 4, 128], BF16)
    qT = p.sb("qT", [128, 4, W], BF16)
    oT = p.sb("oT", [128, 4, W], BF16)
    PT = Rot([p.sb(f"PT{i}", [128, 2, W], BF16) for i in range(2)])
    halo = p.sb("halo", [128, FC * 2, 2], F32)
    convw = p.sb("convw", [128, FC, 2, 3], F32)
    convb = p.sb("convb", [128, FC, 2], F32)
    lng = p.sb("lng", [128, 3, KC], F32)
    lnb = p.sb("lnb", [128, 3, KC], F32)
    hmask = p.sb("hmask", [128, 1], F32)
    onesf = p.sb("onesf", [128, 128], F32)
    onesb = p.sb("onesb", [128, 128], BF16)
    psA = Rot([p.ps(f"psA{i}", [128, 512]) for i in range(6)])
    psB = Rot([p.ps(f"psB{i}", [128, 512]) for i in range(2)])
    hb = Rot([p.sb(f"hb{i}", [128, W + 2], F32) for i in range(4)])
    tmp = Rot([p.sb(f"tmp{i}", [128, W], F32) for i in range(8)])
    stat = Rot([p.sb(f"stat{i}", [128, W], F32) for i in range(4)])

    p.dma(p.sp, convw[:], d["convw"], outs=[convw])
    p.dma(p.sp, convb[:], d["convb"], outs=[convb])
    p.dma(p.sp, lng[:], d["lng"], outs=[lng])
    p.dma(p.sp, lnb[:], d["lnb"], outs=[lnb])
    p.dma(p.sp, hmask[:], d["hmask"], outs=[hmask])
    p.op(p.dve, lambda h: h.memset(onesf[:], 1.0 / D), outs=[onesf])
    p.op(p.dve, lambda h: h.memset(onesb[:], 1.0), outs=[onesb])

    def mm_group(ps, n, lhs, rhs, width, fp32=False):
        for k in range(n):
            p.op(p.pe, lambda h, k=k: h.matmul(ps[:, 0:width], lhs(k)[0], rhs(k)[0], start=(k == 0), stop=(k == n - 1)),
                 ins=[lhs(k)[1], rhs(k)[1]], outs=[ps], signal=(k == n - 1))

    memT = gpart
    p.dma(p.pool, memT[:, 0:KC * NMEM], d["memT"].rearrange("p k m -> p (k m)"), outs=[memT])
    for hh in range(4):
        wb = ws.next(d["cak"][hh], 32)
        ps = psA.get()
        mm_group(ps, KC, lambda k: (wb[:, k * 128:(k + 1) * 128], wb),
                 lambda k: (memT[:, k * NMEM:(k + 1) * NMEM], memT), NMEM)
        p.op(p.act, lambda h: h.activation(kT[:, hh, :], ps[:, 0:NMEM], AF.Copy), ins=[ps], outs=[kT])
    for hh in range(4):
        wb = ws.next(d["cav"][hh], 32)
        for mc in range(2):
            ps = psA.get()
            mm_group(ps, KC, lambda k: (memT[:, k * NMEM + mc * 128:k * NMEM + (mc + 1) * 128], memT),
                     lambda k: (wb[:, k * 128:(k + 1) * 128], wb), 128)
            p.op(p.act, lambda h: h.activation(vm[:, mc, hh, :], ps[:, 0:128], AF.Copy), ins=[ps], outs=[vm])

    def evac_resid(ps, m, first=True):
        if first:
            p.op(p.dve, lambda h: h.scalar_tensor_tensor(xres[m][:], xres[m][:], ALPHA, ps[:, 0:W], ALU.mult, ALU.add),
                 ins=[ps, xres[m]], outs=[xres[m]])
        else:
            p.op(p.dve, lambda h: h.tensor_tensor(xres[m][:], xres[m][:], ps[:, 0:W], ALU.add),
                 ins=[ps, xres[m]], outs=[xres[m]])

    def layer_norm(idx, want_bf=True):
        psm = psB.get()
        mm_group(psm, KC, lambda k: (onesf[:], onesf), lambda k: (xres[k][:], xres[k]), W)
        mean = stat.get()
        p.op(p.act, lambda h: h.activation(mean[:], psm[:, 0:W], AF.Copy), ins=[psm], outs=[mean])
        psv = psB.get()
        for k in range(KC):
            p.op(p.dve, lambda h, k=k: h.tensor_tensor(xres[k][:], xres[k][:], mean[:], ALU.subtract),
                 ins=[xres[k], mean], outs=[xres[k]])
            sq = tmp.get()
            p.op(p.act, lambda h, k=k, sq=sq: h.activation(sq[:], xres[k][:], AF.Square), ins=[xres[k]], outs=[sq])
            p.op(p.pe, lambda h, k=k, sq=sq: h.matmul(psv[:, 0:W], onesf[:], sq[:], start=(k == 0), stop=(k == KC - 1)),
                 ins=[onesf, sq], outs=[psv], signal=True)
        rstd = stat.get()
        p.op(p.dve, lambda h: h.tensor_scalar(rstd[:], psv[:, 0:W], LN_EPS, None, ALU.add), ins=[psv], outs=[rstd])
        p.op(p.act, lambda h: h.activation(rstd[:], rstd[:], AF.Sqrt), ins=[rstd], outs=[rstd])
        p.op(p.dve, lambda h: h.reciprocal(rstd[:], rstd[:]), ins=[rstd], outs=[rstd])
        for k in range(KC):
            p.op(p.dve, lambda h, k=k: h.tensor_tensor(xres[k][:], xres[k][:], rstd[:], ALU.mult),
                 ins=[xres[k], rstd], outs=[xres[k]])
            p.op(p.pool, lambda h, k=k: h.tensor_scalar(xres[k][:], xres[k][:], lng[:, idx, k:k + 1], lnb[:, idx, k:k + 1],
                                                       ALU.mult, ALU.add),
                 ins=[xres[k], lng, lnb], outs=[xres[k]])
            if want_bf:
                p.op(p.act, lambda h, k=k: h.activation(xbf[k][:], xres[k][:], AF.Copy), ins=[xres[k]], outs=[xbf[k]])

    xTv = d["xT"]
    mixv = d["mixT"]
    yv = d["yT"]
    for ti in range(ntiles):
        c0 = ti * W
        p.dma(p.sp, xres_all[:], xTv[:, :, c0:c0 + W], outs=xres)
        p.dma(p.sp, xbf_all[:], mixv[:, :, c0:c0 + W], outs=xbf)
        for m in range(KC):
            wb = ws.next(d["wo"][m], 32)
            ps = psA.get()
            mm_group(ps, KC, lambda k: (wb[:, k * 128:(k + 1) * 128], wb), lambda k: (xbf[k][:], xbf[k]), W)
            evac_resid(ps, m)
        layer_norm(0)
        for hh in range(4):
            wb = ws.next(d["caq"][hh], 32)
            ps = psA.get()
            mm_group(ps, KC, lambda k: (wb[:, k * 128:(k + 1) * 128], wb), lambda k: (xbf[k][:], xbf[k]), W)
            p.op(p.act, lambda h: h.activation(qT[:, hh, :], ps[:, 0:W], AF.Copy), ins=[ps], outs=[qT])
        for hh in range(4):
            pt = PT.get()
            for mc in range(2):
                ps = psA.get()
                p.op(p.pe, lambda h: h.matmul(ps[:, 0:W], kT[:, hh, mc * 128:(mc + 1) * 128], qT[:, hh, :], start=True, stop=True),
                     ins=[kT, qT], outs=[ps])
                p.op(p.act, lambda h: h.activation(pt[:, mc, :], ps[:, 0:W], AF.Exp, scale=float(128 ** -0.5)),
                     ins=[ps], outs=[pt])
            psl = psB.get()
            mm_group(psl, 2, lambda k: (onesb[:], onesb), lambda k: (pt[:, k, :], pt), W)
            rl = stat.get()
            p.op(p.dve, lambda h: h.reciprocal(rl[:], psl[:, 0:W]), ins=[psl], outs=[rl])
            pso = psA.get()
            mm_group(pso, 2, lambda k: (vm[:, k, hh, :], vm), lambda k: (pt[:, k, :], pt), W)
            p.op(p.dve, lambda h: h.tensor_tensor(oT[:, hh, :], pso[:, 0:W], rl[:], ALU.mult), ins=[pso, rl], outs=[oT])
        for m in range(KC):
            wb = ws.next(d["cao"][m], 4)
            ps = psA.get()
            mm_group(ps, 4, lambda k: (wb[:, k * 128:(k + 1) * 128], wb), lambda k: (oT[:, k, :], oT), W)
            evac_resid(ps, m)
        layer_norm(1)
        for pi, (f0, f1) in enumerate(FPARTS):
            for j in range(f0, f1):
                tcv = []
                for half, wname in ((0, "wupg"), (1, "wupv")):
                    wb = ws.next(d[wname][j], 32)
                    ps = psA.get()
                    mm_group(ps, KC, lambda k: (wb[:, k * 128:(k + 1) * 128], wb), lambda k: (xbf[k][:], xbf[k]), W)
                    hbuf = hb.get()
                    p.op(p.act, lambda h: h.activation(hbuf[:, 2:W + 2], ps[:, 0:W], AF.Copy), ins=[ps], outs=[hbuf])
                    hi = j * 2 + half
                    if ti == 0:
                        p.op(p.pool, lambda h: h.memset(hbuf[:, 0:2], 0.0), outs=[hbuf])
                        p.op(p.pool, lambda h: h.tensor_scalar(hbuf[:, 2:4], hbuf[:, 2:4], hmask[:, 0:1], None, ALU.mult),
                             ins=[hmask, hbuf], outs=[hbuf])
                    else:
                        p.op(p.pool, lambda h: h.tensor_copy(hbuf[:, 0:2], halo[:, hi, :]), ins=[halo], outs=[hbuf])
                    if ti < ntiles - 1:
                        p.op(p.pool, lambda h: h.tensor_copy(halo[:, hi, :], hbuf[:, W:W + 2]), ins=[hbuf], outs=[halo])
                    t = tmp.get()
                    p.op(p.dve, lambda h: h.tensor_scalar(t[:], hbuf[:, 2:W + 2], convw[:, j, half, 2:3], convb[:, j, half:half + 1],
                                                         ALU.mult, ALU.add), ins=[hbuf, convw, convb], outs=[t])
                    p.op(p.dve, lambda h: h.scalar_tensor_tensor(t[:], hbuf[:, 1:W + 1], convw[:, j, half, 1:2], t[:], ALU.mult, ALU.add),
                         ins=[hbuf, convw, t], outs=[t])
                    p.op(p.dve, lambda h: h.scalar_tensor_tensor(t[:], hbuf[:, 0:W], convw[:, j, half, 0:1], t[:], ALU.mult, ALU.add),
                         ins=[hbuf, convw, t], outs=[t])
                    tcv.append(t)
                sg = tmp.get()
                p.op(p.act, lambda h: h.activation(sg[:], tcv[0][:], AF.Silu), ins=[tcv[0]], outs=[sg])
                jj = j - f0
                p.op(p.pool, lambda h: h.tensor_tensor(gpart[:, jj * W:(jj + 1) * W], sg[:], tcv[1][:], ALU.mult),
                     ins=[sg, tcv[1]], outs=[gpart])
            nf = f1 - f0
            for m in range(KC):
                wb = ws.next(d["wdn"][m][:, f0:f1, :], nf)
                ps = psA.get()
                mm_group(ps, nf, lambda k: (wb[:, k * 128:(k + 1) * 128], wb), lambda k: (gpart[:, k * W:(k + 1) * W], gpart), W)
                evac_resid(ps, m, first=(pi == 0))
        layer_norm(2, want_bf=False)
        p.dma(p.sp, yv[:, :, c0:c0 + W], xres_all[:], ins=xres, owner=xres[0])
    return xres


def build_pm(ntiles=NT):
    nc = bass.Bass("TRN2", target_bir_lowering=False)
    d = declare_pm(nc)
    dry = DryP()
    ws0 = WStream(dry, NB)
    emit_pm(dry, d, ws0, ntiles)
    p = P(nc)
    ws = WStream(p, NB, seq=ws0.seq)
    xres = emit_pm(p, d, ws, ntiles)
    p.finish(xres)
    print("pm stats", p.stats(), "panels", len(ws0.seq))
    return nc


import math

SEQ = 8192
QT = 256
LAM_INIT0 = 0.8 - 0.6 * math.exp(0.0)
SCALE = float(128 ** -0.5)
NEG = -30000.0
THR = [0] + list(range(1, 17)) + [int(math.ceil(16 * 8 ** ((b - 16) / 16.0))) for b in range(17, 32)]


def declare_att(nc, ntok):
    d = {}

    def inp(name, shape, dt=F32):
        d[name] = nc.dram_tensor(name, list(shape), dt, kind="ExternalInput").ap()

    inp("xT", [128, KC, ntok])
    inp("wqkv", [2, 128, KC, 768])
    inp("relb", [128, 32, 2])
    inp("lam", [128, 4, 128])
    inp("subw", [128, 256])
    inp("nidx", [128, 512])
    d["o"] = nc.dram_tensor("o", [2, ntok, 256], BF16, kind="ExternalOutput").ap()
    return d


def emit_att(p, d, seq, nbatch=2):
    nqt = seq // QT
    nblk = seq // 128
    W = [p.sb(f"W{h}", [128, KC, 768], BF16) for h in range(2)]
    KT = p.sb("KT", [128, 2, seq], BF16)
    Vc = p.sb("Vc", [128, nblk, 257], BF16)
    xch = p.sb("xch", [128, KC, QT], BF16)
    qT = p.sb("qT", [128, 2, QT], BF16)
    relb = p.sb("relb", [128, 32, 2], F32)
    dtab = p.sb("dtab", [128, 31, 2], F32)
    lamr = p.sb("lamr", [128, 4, 128], F32)
    subw = p.sb("subw", [128, 256], F32)
    nidx = p.sb("nidx", [128, 512], F32)
    biasall = [p.sb(f"bias{h}", [128, 512], F32) for h in range(2)]
    b31 = p.sb("b31", [128, 2], F32)
    neglam = p.sb("neglam", [128, 1], F32)
    small = p.sb("small", [128, 8], F32)
    psS = Rot([p.ps(f"psS{i}", [128, 512]) for i in range(4)])
    acc = [[p.ps(f"acc{m}{q}", [128, 512]) for q in range(2)] for m in range(2)]
    pts = Rot([p.sb(f"pt{i}", [128, QT], BF16) for i in range(4)])
    tmpf = Rot([p.sb(f"tf{i}", [128, 512], F32) for i in range(4)])
    sm = Rot([p.sb(f"sm{i}", [128, 4], F32) for i in range(8)])
    ob = Rot([p.sb(f"ob{i}", [128, 256], BF16) for i in range(2)])

    for h in range(2):
        for part in range(3):
            p.dma(p.pool, W[h][:, :, part * 256:(part + 1) * 256], d["wqkv"][h][:, :, part * 256:(part + 1) * 256], outs=[W[h]])
    p.dma(p.sp, relb[:], d["relb"], outs=[relb])
    p.dma(p.sp, lamr[:], d["lam"], outs=[lamr])
    p.dma(p.sp, subw[:], d["subw"], outs=[subw])
    p.dma(p.sp, nidx[:], d["nidx"], outs=[nidx])
    p.op(p.dve, lambda h: h.memset(Vc[:, :, 256:257], 1.0), outs=[Vc])
    p.op(p.dve, lambda h: h.tensor_tensor(dtab[:], relb[:, 1:32, :], relb[:, 0:31, :], ALU.subtract), ins=[relb], outs=[dtab])
    p.op(p.dve, lambda h: h.tensor_copy(b31[:], relb[:, 31, :]), ins=[relb], outs=[b31])
    for i in range(2):
        t = tmpf.get()
        p.op(p.dve, lambda h: h.tensor_tensor(t[:, 0:128], lamr[:, 2 * i, :], lamr[:, 2 * i + 1, :], ALU.mult), ins=[lamr], outs=[t])
        p.op(p.dve, lambda h: h.tensor_reduce(small[:, i:i + 1], t[:, 0:128], AX.X, ALU.add), ins=[t], outs=[small])
    p.op(p.act, lambda h: h.activation(small[:, 2:4], small[:, 0:2], AF.Exp), ins=[small], outs=[small])
    p.op(p.dve, lambda h: h.tensor_tensor(small[:, 4:5], small[:, 3:4], small[:, 2:3], ALU.subtract), ins=[small], outs=[small])
    p.op(p.dve, lambda h: h.tensor_scalar(neglam[:], small[:, 4:5], -LAM_INIT0, None, ALU.add), ins=[small], outs=[neglam])
    for hh in range(2):
        ba = biasall[hh]
        p.op(p.dve, lambda h: h.tensor_scalar(ba[:], nidx[:], 0.0, NEG, ALU.is_lt, ALU.mult), ins=[nidx], outs=[ba])
        p.op(p.dve, lambda h: h.tensor_scalar(ba[:], ba[:], relb[:, 0, hh:hh + 1], None, ALU.add), ins=[ba, relb], outs=[ba])
        for b in range(1, 32):
            t = tmpf.get()
            p.op(p.dve, lambda h: h.tensor_scalar(t[:], nidx[:], float(THR[b]), dtab[:, b - 1, hh:hh + 1], ALU.is_ge, ALU.mult),
                 ins=[nidx, dtab], outs=[t])
            p.op(p.pool, lambda h: h.tensor_tensor(ba[:], ba[:], t[:], ALU.add), ins=[ba, t], outs=[ba])

    xv = d["xT"]
    for b in range(nbatch):
        for hh in range(2):
            Wh = W[hh]
            for qt in range(nqt):
                tok0 = b * seq + qt * QT
                p.dma(p.pool, xch[:], xv[:, :, tok0:tok0 + QT], outs=[xch])
                for f in range(4):
                    ps = psS.get()
                    for k in range(KC):
                        p.op(p.pe, lambda h: h.matmul(ps[:, 0:QT], Wh[:, k, f * 128:(f + 1) * 128], xch[:, k, :],
                                                      start=(k == 0), stop=(k == KC - 1)),
                             ins=[Wh, xch], outs=[ps], signal=(k == KC - 1))
                    if f < 2:
                        p.op(p.act, lambda h: h.activation(qT[:, f, :], ps[:, 0:QT], AF.Copy), ins=[ps], outs=[qT])
                    else:
                        p.op(p.act, lambda h: h.activation(KT[:, f - 2, qt * QT:(qt + 1) * QT], ps[:, 0:QT], AF.Copy),
                             ins=[ps], outs=[KT])
                for blk in range(2):
                    ps = psS.get()
                    for k in range(KC):
                        p.op(p.pe, lambda h: h.matmul(ps[:, 0:256], xch[:, k, blk * 128:(blk + 1) * 128], Wh[:, k, 512:768],
                                                      start=(k == 0), stop=(k == KC - 1)),
                             ins=[Wh, xch], outs=[ps], signal=(k == KC - 1))
                    p.op(p.dve, lambda h: h.tensor_copy(Vc[:, 2 * qt + blk, 0:256], ps[:, 0:256]), ins=[ps], outs=[Vc])
                nkb = 2 * qt + 2
                for kb in range(nkb):
                    rel = kb - 2 * qt
                    for m in range(2):
                        ps = psS.get()
                        p.op(p.pe, lambda h: h.matmul(ps[:, 0:QT], KT[:, m, kb * 128:(kb + 1) * 128], qT[:, m, :], start=True, stop=True),
                             ins=[KT, qT], outs=[ps])
                        pt = pts.get()
                        if rel < -1:
                            p.op(p.act, lambda h: h.activation(pt[:], ps[:, 0:QT], AF.Exp, bias=b31[:, hh:hh + 1], scale=SCALE),
                                 ins=[ps, b31], outs=[pt])
                        else:
                            t = tmpf.get()
                            off = 128 - rel * 128
                            p.op(p.dve, lambda h: h.scalar_tensor_tensor(t[:, 0:QT], ps[:, 0:QT], SCALE, biasall[hh][:, off:off + QT],
                                                                         ALU.mult, ALU.add), ins=[ps, biasall[hh]], outs=[t])
                            p.op(p.act, lambda h: h.activation(pt[:], t[:, 0:QT], AF.Exp), ins=[t], outs=[pt])
                        for qb in range(2):
                            last = 2 * qt + qb
                            if kb <= last:
                                a = acc[m][qb]
                                p.op(p.pe, lambda h: h.matmul(a[:, 0:257], pt[:, qb * 128:(qb + 1) * 128], Vc[:, kb, :],
                                                              start=(kb == 0), stop=(kb == last)),
                                     ins=[pt, Vc], outs=[a])
                for qb in range(2):
                    a1, a2 = acc[0][qb], acc[1][qb]
                    s = sm.get()
                    p.op(p.dve, lambda h: h.reciprocal(s[:, 0:1], a1[:, 256:257]), ins=[a1], outs=[s])
                    p.op(p.dve, lambda h: h.reciprocal(s[:, 1:2], a2[:, 256:257]), ins=[a2], outs=[s])
                    p.op(p.dve, lambda h: h.tensor_tensor(s[:, 1:2], s[:, 1:2], neglam[:], ALU.mult), ins=[s, neglam], outs=[s])
                    o1 = tmpf.get()
                    p.op(p.dve, lambda h: h.tensor_scalar(o1[:, 0:256], a1[:, 0:256], s[:, 0:1], None, ALU.mult), ins=[a1, s], outs=[o1])
                    p.op(p.dve, lambda h: h.scalar_tensor_tensor(o1[:, 0:256], a2[:, 0:256], s[:, 1:2], o1[:, 0:256], ALU.mult, ALU.add),
                         ins=[a2, s, o1], outs=[o1])
                    sq = tmpf.get()
                    p.op(p.dve, lambda h: h.memset(s[:, 2:3], 0.0), outs=[s])
                    p.op(p.act, lambda h: h.activation(sq[:, 0:256], o1[:, 0:256], AF.Square, accum_out=s[:, 2:3]), ins=[o1], outs=[sq, s])
                    p.op(p.dve, lambda h: h.tensor_scalar(s[:, 2:3], s[:, 2:3], 1.0 / 256, 1e-5, ALU.mult, ALU.add), ins=[s], outs=[s])
                    p.op(p.act, lambda h: h.activation(s[:, 2:3], s[:, 2:3], AF.Sqrt), ins=[s], outs=[s])
                    p.op(p.dve, lambda h: h.reciprocal(s[:, 3:4], s[:, 2:3]), ins=[s], outs=[s])
                    p.op(p.dve, lambda h: h.tensor_scalar(s[:, 3:4], s[:, 3:4], 1.0 - LAM_INIT0, None, ALU.mult), ins=[s], outs=[s])
                    o2 = ob.get()
                    p.op(p.dve, lambda h: h.scalar_tensor_tensor(o2[:], o1[:, 0:256], s[:, 3:4], subw[:], ALU.mult, ALU.mult),
                         ins=[o1, s, subw], outs=[o2])
                    r0 = tok0 + qb * 128
                    p.dma(p.sp, d["o"][hh, r0:r0 + 128, :], o2[:], ins=[o2])
    return [ob.t[0], ob.t[1]]


def build_att(seq=SEQ, nbatch=2):
    nc = bass.Bass("TRN2", target_bir_lowering=False)
    d = declare_att(nc, seq * nbatch)
    p = P(nc)
    outs = emit_att(p, d, seq, nbatch)
    p.finish(outs)
    print("att stats", p.stats())
    return nc


import math

SEQ = 8192
HC = 512
GN_EPS = 64e-5
TC = 64
SCAN_SELFSYNC = True
R1_OUT = ["R", "Wd", "K", "V", "A", "B", "G"]


def declare_r1(nc, seq, nbatch):
    d = {}
    ntok = seq * nbatch

    def inp(name, shape, dt=F32):
        d[name] = nc.dram_tensor(name, list(shape), dt, kind="ExternalInput").ap()

    inp("x1T", [128, KC, nbatch * (seq + 1)])
    inp("mix", [128, 6, KC])
    for n in ("wr", "wk", "wv"):
        inp(n, [128, KC, HC])
    inp("w1", [128, KC, 128])
    inp("a1", [128, KC, 128])
    inp("g1", [128, KC, 480])
    inp("w2s", [128, HC])
    inp("a2s", [128, HC])
    inp("g2s", [120, 4, HC])
    for n in ("w0", "a0", "kkv", "kav", "rkv"):
        inp(n, [128, HC])
    for n in R1_OUT:
        d[n] = nc.dram_tensor(n, [ntok, HC], F32, kind="ExternalOutput").ap()
    d["RKR"] = nc.dram_tensor("RKR", [ntok, 8], F32, kind="ExternalOutput").ap()
    return d


TB = 64


def emit_r1(p, d, seq, nbatch):
    Wr, Wk, Wv = [p.sb(n, [128, KC, HC], BF16) for n in ("Wr", "Wk", "Wv")]
    W1 = p.sb("W1", [128, KC, 128], BF16)
    A1 = p.sb("A1", [128, KC, 128], BF16)
    G1 = p.sb("G1", [128, KC, 480], BF16)
    W2 = p.sb("W2", [128, HC], BF16)
    A2 = p.sb("A2", [128, HC], BF16)
    G2 = p.sb("G2", [120, 4, HC], BF16)
    mix = p.sb("mix", [128, 6, KC], F32)
    omm = p.sb("omm", [128, 6, KC], F32)
    vec = {n: p.sb("v_" + n, [128, HC], F32) for n in ("w0", "a0", "kkv", "kav", "rkv")}
    xf = p.sb("xf", [128, KC, TB + 1], F32)
    xmr = Rot([p.sb(f"xm{i}", [128, KC, TB], BF16) for i in range(2)])
    hT = Rot([p.sb(f"hT{i}", [128, TB], BF16) for i in range(3)])
    hgT = p.sb("hgT", [120, 4, TB], BF16)
    psA = Rot([p.ps(f"psA{i}", [128, 512]) for i in range(6)])
    psB = Rot([p.ps(f"psB{i}", [128, 512]) for i in range(2)])
    tf = Rot([p.sb(f"tf{i}", [TB, HC], F32) for i in range(14)])
    sm = Rot([p.sb(f"sm{i}", [TB, 8], F32) for i in range(6)])

    for (t, n) in ((Wr, "wr"), (Wk, "wk"), (Wv, "wv"), (W1, "w1"), (A1, "a1"), (G1, "g1"), (W2, "w2s"), (A2, "a2s"), (G2, "g2s")):
        p.dma(p.pool, t[:], d[n], outs=[t])
    p.dma(p.sp, mix[:], d["mix"], outs=[mix])
    p.op(p.dve, lambda h: h.tensor_scalar(omm[:], mix[:], -1.0, 1.0, ALU.mult, ALU.add), ins=[mix], outs=[omm])
    for n, t in vec.items():
        p.dma(p.sp, t[:], d[n], outs=[t])
    V = {n: t[0:TB, :] for n, t in vec.items()}

    nblk = seq // TB
    for b in range(nbatch):
        for blk in range(nblk):
            c0 = b * (seq + 1) + blk * TB
            r0 = b * seq + blk * TB
            p.dma(p.sp, xf[:], d["x1T"][:, :, c0:c0 + TB + 1], outs=[xf])
            cnt = [0]

            def mixed(i):
                xm = xmr.get()
                mb = mix[:, i, :].rearrange("p (k o) -> p k o", o=1).broadcast_to([128, KC, TB])
                ob = omm[:, i, :].rearrange("p (k o) -> p k o", o=1).broadcast_to([128, KC, TB])
                cnt[0] += 1
                eng = p.dve if cnt[0] % 2 == 0 else p.pool
                tmpx = xmr.get()
                p.op(eng, lambda h: h.tensor_tensor(tmpx[:], xf[:, :, 0:TB], mb, ALU.mult), ins=[xf, mix], outs=[tmpx])
                p.op(eng, lambda h: h.tensor_tensor(xm[:], xf[:, :, 1:TB + 1], ob, ALU.mult), ins=[xf, omm], outs=[xm])
                p.op(eng, lambda h: h.tensor_tensor(xm[:], xm[:], tmpx[:], ALU.add), ins=[xm, tmpx], outs=[xm])
                return xm

            def proj(i, Wt):
                xin = mixed(i)
                ps = psA.get()
                for k in range(KC):
                    p.op(p.pe, lambda h: h.matmul(ps[0:TB, 0:HC], xin[:, k, :], Wt[:, k, :], start=(k == 0), stop=(k == KC - 1)),
                         ins=[xin, Wt], outs=[ps], signal=(k == KC - 1))
                return ps

            def lora1(xin, Wt, c0_, ncol, func, dst, dst_ap):
                ps = psB.get()
                for k in range(KC):
                    p.op(p.pe, lambda h: h.matmul(ps[0:ncol, 0:TB], Wt[:, k, c0_:c0_ + ncol], xin[:, k, :], start=(k == 0), stop=(k == KC - 1)),
                         ins=[xin, Wt], outs=[ps], signal=(k == KC - 1))
                p.op(p.act, lambda h: h.activation(dst_ap, ps[0:ncol, 0:TB], func), ins=[ps], outs=[dst])

            ps_r = proj(0, Wr)
            rs = tf.get()
            p.op(p.act, lambda h: h.activation(rs[:], ps_r[0:TB, 0:HC], AF.Copy), ins=[ps_r], outs=[rs])
            ps_v = proj(3, Wv)
            vs = tf.get()
            p.op(p.act, lambda h: h.activation(vs[:], ps_v[0:TB, 0:HC], AF.Copy), ins=[ps_v], outs=[vs])
            ps_k = proj(2, Wk)
            ks = tf.get()
            p.op(p.act, lambda h: h.activation(ks[:], ps_k[0:TB, 0:HC], AF.Copy), ins=[ps_k], outs=[ks])
            hw = hT.get()
            lora1(mixed(1), W1, 0, 128, AF.Tanh, hw, hw[:])
            ps_w = psA.get()
            p.op(p.pe, lambda h: h.matmul(ps_w[0:TB, 0:HC], hw[:], W2[:], start=True, stop=True), ins=[hw, W2], outs=[ps_w])
            zt = tf.get()
            p.op(p.dve, lambda h: h.tensor_tensor(zt[:], ps_w[0:TB, 0:HC], V["w0"], ALU.add), ins=[ps_w, vec["w0"]], outs=[zt])
            p.op(p.act, lambda h: h.activation(zt[:], zt[:], AF.Sigmoid), ins=[zt], outs=[zt])
            wd = tf.get()
            p.op(p.act, lambda h: h.activation(wd[:], zt[:], AF.Exp, scale=-math.exp(-0.5)), ins=[zt], outs=[wd])
            ha = hT.get()
            lora1(mixed(4), A1, 0, 128, AF.Copy, ha, ha[:])
            ps_a = psA.get()
            p.op(p.pe, lambda h: h.matmul(ps_a[0:TB, 0:HC], ha[:], A2[:], start=True, stop=True), ins=[ha, A2], outs=[ps_a])
            at = tf.get()
            p.op(p.dve, lambda h: h.tensor_tensor(at[:], ps_a[0:TB, 0:HC], V["a0"], ALU.add), ins=[ps_a, vec["a0"]], outs=[at])
            p.op(p.act, lambda h: h.activation(at[:], at[:], AF.Sigmoid), ins=[at], outs=[at])
            xg = mixed(5)
            for gi in range(4):
                lora1(xg, G1, gi * 120, 120, AF.Sigmoid, hgT, hgT[:, gi, :])
            ps_g = psA.get()
            for gi in range(4):
                p.op(p.pe, lambda h: h.matmul(ps_g[0:TB, 0:HC], hgT[:, gi, :], G2[:, gi, :], start=(gi == 0), stop=(gi == 3)),
                     ins=[hgT, G2], outs=[ps_g], signal=(gi == 3))
            gs = tf.get()
            p.op(p.act, lambda h: h.activation(gs[:], ps_g[0:TB, 0:HC], AF.Copy), ins=[ps_g], outs=[gs])
            v3 = lambda t: t[:].rearrange("p (a b) -> p a b", a=8)
            kk = tf.get()
            p.op(p.dve, lambda h: h.tensor_tensor(kk[:], ks[:], V["kkv"], ALU.mult), ins=[ks, vec["kkv"]], outs=[kk])
            sq = tf.get()
            p.op(p.act, lambda h: h.activation(sq[:], kk[:], AF.Square), ins=[kk], outs=[sq])
            s1 = sm.get()
            p.op(p.dve, lambda h: h.tensor_reduce(s1[:], v3(sq), AX.X, ALU.add), ins=[sq], outs=[s1])
            p.op(p.act, lambda h: h.activation(s1[:], s1[:], AF.Sqrt), ins=[s1], outs=[s1])
            p.op(p.dve, lambda h: h.tensor_scalar(s1[:], s1[:], 1e-12, None, ALU.max), ins=[s1], outs=[s1])
            p.op(p.dve, lambda h: h.reciprocal(s1[:], s1[:]), ins=[s1], outs=[s1])
            s1b = s1[:].rearrange("p (a o) -> p a o", o=1).broadcast_to([TB, 8, 64])
            p.op(p.dve, lambda h: h.tensor_tensor(v3(kk), v3(kk), s1b, ALU.mult), ins=[kk, s1], outs=[kk])
            An = tf.get()
            p.op(p.pool, lambda h: h.tensor_scalar(An[:], kk[:], -1.0, None, ALU.mult), ins=[kk], outs=[An])
            Bn = tf.get()
            p.op(p.pool, lambda h: h.tensor_tensor(Bn[:], kk[:], at[:], ALU.mult), ins=[kk, at], outs=[Bn])
            t1 = tf.get()
            p.op(p.dve, lambda h: h.scalar_tensor_tensor(t1[:], at[:], -1.0, V["kav"], ALU.add, ALU.mult), ins=[at, vec["kav"]], outs=[t1])
            km = tf.get()
            p.op(p.dve, lambda h: h.scalar_tensor_tensor(km[:], t1[:], 1.0, ks[:], ALU.add, ALU.mult), ins=[t1, ks], outs=[km])
            t2 = tf.get()
            p.op(p.pool, lambda h: h.tensor_tensor(t2[:], rs[:], km[:], ALU.mult), ins=[rs, km], outs=[t2])
            p.op(p.pool, lambda h: h.tensor_tensor(t2[:], t2[:], V["rkv"], ALU.mult), ins=[t2, vec["rkv"]], outs=[t2])
            s2 = sm.get()
            p.op(p.dve, lambda h: h.tensor_reduce(s2[:], v3(t2), AX.X, ALU.add), ins=[t2], outs=[s2])
            for (n, t) in (("R", rs), ("Wd", wd), ("K", km), ("V", vs), ("A", An), ("B", Bn), ("G", gs)):
                p.dma(p.sp, d[n][r0:r0 + TB, :], t[:], ins=[t])
            p.dma(p.sp, d["RKR"][r0:r0 + TB, :], s2[:], ins=[s2])
    return tf.t + sm.t


def build_r1(seq=SEQ, nbatch=2):
    nc = bass.Bass("TRN2", target_bir_lowering=False)
    d = declare_r1(nc, seq, nbatch)
    p = P(nc)
    outs = emit_r1(p, d, seq, nbatch)
    p.finish(outs)
    print("r1 stats", p.stats())
    return nc


def declare_r2(nc, seq, nbatch):
    d = {}
    ntok = seq * nbatch

    def inp(name, shape, dt=F32):
        d[name] = nc.dram_tensor(name, list(shape), dt, kind="ExternalInput").ap()

    for n in R1_OUT:
        inp(n, [ntok, HC])
    inp("RKR", [ntok, 8])
    inp("lnw", [128, HC])
    inp("lnb", [128, HC])
    d["Y"] = nc.dram_tensor("Y", [ntok, HC], F32, kind="Internal").ap()
    d["o"] = nc.dram_tensor("o", [ntok, HC], BF16, kind="ExternalOutput").ap()
    return d


def emit_r2(p, d, seq, nbatch):
    assert nbatch == 2
    S = p.sb("S", [128, 8, 64], F32)
    names = ["A", "Wd", "B", "K", "R"]
    bufs = {n: [p.sb(f"{n}{i}", [128, TC, 64], F32) for i in range(2)] for n in names}
    vbuf = [p.sb(f"Vb{i}", [128, TC, 8], F32) for i in range(2)]
    ybuf = [p.sb(f"Yb{i}", [128, TC, 8], F32) for i in range(2)]
    tmp = Rot([p.sb(f"tmp{i}", [128, 8, 64], F32) for i in range(3)])
    t2r = Rot([p.sb(f"t2_{i}", [128, 8, 64], F32) for i in range(2)])
    t3r = Rot([p.sb(f"t3_{i}", [128, 8, 64], F32) for i in range(3)])
    sar = Rot([p.sb(f"sa{i}", [128, 8], F32) for i in range(3)])
    Ydram = p.view("Ydram", d["Y"])
    p.op(p.dve, lambda h: h.memset(S[:], 0.0), outs=[S])
    nchunk = seq // TC
    qs = [p.sp, p.act]
    qi = [0]

    def q():
        qi[0] += 1
        return qs[qi[0] % 2]

    def load(ci):
        t0 = ci * TC
        s = ci % 2
        for n in names:
            for pr in range(16):
                b, hl = pr // 8, pr % 8
                src = d[n][b * seq + t0:b * seq + t0 + TC, hl * 64:(hl + 1) * 64]
                p.dma(q(), bufs[n][s][pr * 8:(pr + 1) * 8, :, :],
                      src.rearrange("(o t) k -> o t k", o=1).broadcast_to([8, TC, 64]), outs=[bufs[n][s]])
        for pr in range(16):
            b, hl = pr // 8, pr % 8
            src = d["V"][b * seq + t0:b * seq + t0 + TC, hl * 64:(hl + 1) * 64]
            p.dma(q(), vbuf[s][pr * 8:(pr + 1) * 8, :, :], src.rearrange("t (vb vi) -> vb t vi", vb=8), outs=[vbuf[s]])

    def bc(ap2):
        return ap2.rearrange("p (o k) -> p o k", o=1).broadcast_to([128, 8, 64])

    def bk(ap2):
        return ap2.rearrange("p (v o) -> p v o", o=1).broadcast_to([128, 8, 64])

    load(0)
    p.dve.selfsync = SCAN_SELFSYNC
    for ci in range(nchunk):
        s = ci % 2
        if ci + 1 < nchunk:
            load(ci + 1)
        Ab, Wb, Bb, Kb, Rb = (bufs[n][s] for n in names)
        Vb, Yb = vbuf[s], ybuf[s]
        for i in range(TC):
            t3 = t3r.get()
            p.op(p.dve, lambda h: h.tensor_tensor(t3[:], bk(Vb[:, i, :]), bc(Kb[:, i, :]), ALU.mult), ins=[Vb, Kb], outs=[t3])
            t1 = tmp.get()
            p.op(p.dve, lambda h: h.tensor_tensor(t1[:], S[:], bc(Ab[:, i, :]), ALU.mult), ins=[S, Ab], outs=[t1])
            sa = sar.get()
            p.op(p.dve, lambda h: h.tensor_reduce(sa[:], t1[:], AX.X, ALU.add), ins=[t1], outs=[sa])
            p.op(p.dve, lambda h: h.tensor_tensor(S[:], S[:], bc(Wb[:, i, :]), ALU.mult), ins=[S, Wb], outs=[S])
            t2 = t2r.get()
            p.op(p.dve, lambda h: h.tensor_tensor(t2[:], bk(sa[:]), bc(Bb[:, i, :]), ALU.mult), ins=[sa, Bb], outs=[t2])
            p.op(p.dve, lambda h: h.tensor_tensor(S[:], S[:], t3[:], ALU.add), ins=[S, t3], outs=[S])
            p.op(p.dve, lambda h: h.tensor_tensor(S[:], S[:], t2[:], ALU.add), ins=[S, t2], outs=[S])
            t4 = tmp.get()
            p.op(p.dve, lambda h: h.tensor_tensor(t4[:], S[:], bc(Rb[:, i, :]), ALU.mult), ins=[S, Rb], outs=[t4])
            p.op(p.dve, lambda h: h.tensor_reduce(Yb[:, i, :], t4[:], AX.X, ALU.add), ins=[t4], outs=[Yb])
        t0 = ci * TC
        for pr in range(16):
            b, hl = pr // 8, pr % 8
            dst = d["Y"][b * seq + t0:b * seq + t0 + TC, hl * 64:(hl + 1) * 64]
            p.dma(q(), dst.rearrange("t (vb vi) -> vb t vi", vb=8), Yb[pr * 8:(pr + 1) * 8, :, :], ins=[Yb], outs=[Ydram], owner=Yb)

    p.dve.selfsync = True
    p._wait(p.sp, p._need([], ybuf))
    lnw = p.sb("lnw", [128, HC], F32)
    lnb = p.sb("lnb", [128, HC], F32)
    p.dma(p.sp, lnw[:], d["lnw"], outs=[lnw])
    p.dma(p.sp, lnb[:], d["lnb"], outs=[lnb])
    tf = Rot([p.sb(f"pf{i}", [128, HC], F32) for i in range(5)])
    sm = Rot([p.sb(f"ps{i}", [128, 8], F32) for i in range(8)])
    obr = Rot([p.sb(f"ob{i}", [128, HC], BF16) for i in range(2)])
    v3 = lambda t: t[:].rearrange("p (a b) -> p a b", a=8)
    b3 = lambda t: t[:].rearrange("p (a o) -> p a o", o=1).broadcast_to([128, 8, 64])
    ntok = seq * nbatch
    for blk in range(ntok // 128):
        r0 = blk * 128
        y = tf.get()
        p.dma(p.sp, y[:], d["Y"][r0:r0 + 128, :], ins=[Ydram], outs=[y])
        vv = tf.get()
        p.dma(p.sp, vv[:], d["V"][r0:r0 + 128, :], outs=[vv])
        gg = tf.get()
        p.dma(p.sp, gg[:], d["G"][r0:r0 + 128, :], outs=[gg])
        rk = sm.get()
        p.dma(p.sp, rk[:], d["RKR"][r0:r0 + 128, :], outs=[rk])
        mu = sm.get()
        p.op(p.dve, lambda h: h.tensor_reduce(mu[:], v3(y), AX.X, ALU.add), ins=[y], outs=[mu])
        p.op(p.dve, lambda h: h.tensor_scalar(mu[:], mu[:], 1.0 / 64, None, ALU.mult), ins=[mu], outs=[mu])
        p.op(p.dve, lambda h: h.tensor_tensor(v3(y), v3(y), b3(mu), ALU.subtract), ins=[y, mu], outs=[y])
        sq = tf.get()
        p.op(p.act, lambda h: h.activation(sq[:], y[:], AF.Square), ins=[y], outs=[sq])
        var = sm.get()
        p.op(p.dve, lambda h: h.tensor_reduce(var[:], v3(sq), AX.X, ALU.add), ins=[sq], outs=[var])
        p.op(p.dve, lambda h: h.tensor_scalar(var[:], var[:], 1.0 / 64, GN_EPS, ALU.mult, ALU.add), ins=[var], outs=[var])
        p.op(p.act, lambda h: h.activation(var[:], var[:], AF.Sqrt), ins=[var], outs=[var])
        p.op(p.dve, lambda h: h.reciprocal(var[:], var[:]), ins=[var], outs=[var])
        p.op(p.dve, lambda h: h.tensor_tensor(v3(y), v3(y), b3(var), ALU.mult), ins=[y, var], outs=[y])
        p.op(p.pool, lambda h: h.tensor_tensor(y[:], y[:], lnw[:], ALU.mult), ins=[y, lnw], outs=[y])
        p.op(p.pool, lambda h: h.tensor_tensor(y[:], y[:], lnb[:], ALU.add), ins=[y, lnb], outs=[y])
        p.op(p.dve, lambda h: h.tensor_tensor(v3(vv), v3(vv), b3(rk), ALU.mult), ins=[vv, rk], outs=[vv])
        p.op(p.dve, lambda h: h.tensor_tensor(y[:], y[:], vv[:], ALU.add), ins=[y, vv], outs=[y])
        ob = obr.get()
        p.op(p.dve, lambda h: h.tensor_tensor(ob[:], y[:], gg[:], ALU.mult), ins=[y, gg], outs=[ob])
        p.dma(p.sp, d["o"][r0:r0 + 128, :], ob[:], ins=[ob])
    return obr.t


def build_r2(seq=SEQ, nbatch=2):
    nc = bass.Bass("TRN2", target_bir_lowering=False)
    d = declare_r2(nc, seq, nbatch)
    p = P(nc)
    outs = emit_r2(p, d, seq, nbatch)
    p.finish(outs)
    print("r2 stats", p.stats())
    return nc


def declare_rw(nc, seq, nbatch):
    d = {}
    ntok = seq * nbatch

    def inp(name, shape, dt=F32):
        d[name] = nc.dram_tensor(name, list(shape), dt, kind="ExternalInput").ap()

    inp("x1T", [128, KC, nbatch * (seq + 1)])
    inp("mix", [128, 6, KC])
    for n in ("wr", "wk", "wv"):
        inp(n, [128, KC, HC])
    inp("w1", [128, KC, 128])
    inp("a1", [128, KC, 128])
    inp("g1", [128, KC, 480])
    inp("w2s", [128, HC])
    inp("a2s", [128, HC])
    inp("g2s", [120, 4, HC])
    for n in ("w0", "a0", "kkv", "kav", "rkv"):
        inp(n, [128, HC])
    inp("lnw", [128, HC])
    inp("lnb", [128, HC])
    for n in R1_OUT:
        d[n] = nc.dram_tensor("scr_" + n, [ntok, HC], F32, kind="Internal").ap()
    d["RKR"] = nc.dram_tensor("scr_RKR", [ntok, 8], F32, kind="Internal").ap()
    d["Y"] = nc.dram_tensor("scr_Y", [ntok, HC], F32, kind="Internal").ap()
    d["o"] = nc.dram_tensor("o", [ntok, HC], BF16, kind="ExternalOutput").ap()
    return d


def build_rw(seq=SEQ, nbatch=2):
    nc = bass.Bass("TRN2", target_bir_lowering=False)
    d = declare_rw(nc, seq, nbatch)
    p = P(nc)
    emit_r1(p, d, seq, nbatch)
    p.barrier_and_free()
    outs = emit_r2(p, d, seq, nbatch)
    p.finish(outs)
    print("rw stats", p.stats())
    return nc


import ml_dtypes
from concourse.bass_utils import run_bass_kernel_spmd

NCORE = 8


def fm(a):
    return np.ascontiguousarray(a.T.reshape(-1, 128, a.shape[0]).transpose(1, 0, 2))


def panels(w):
    K, M = w.shape
    return np.ascontiguousarray(w.reshape(K // 128, 128, M // 128, 128).transpose(2, 1, 0, 3))


def tile_k(w):
    K, M = w.shape
    return np.ascontiguousarray(w.reshape(K // 128, 128, M).transpose(1, 0, 2))


def rep(v):
    return np.ascontiguousarray(np.broadcast_to(v[None], (128, v.shape[0]))).astype(np.float32)


def prep_pm_weights(I, i, mixer_wo):
    F = 11008
    w = {}
    w["wo"] = panels(mixer_wo)
    w["caq"] = panels(I["ca_w_q"][i])
    kv = I["ca_w_kv"][i]
    w["cak"] = panels(kv[:, :512])
    w["cav"] = panels(kv[:, 512:])
    w["cao"] = panels(I["ca_w_o"][i])
    up = I["ffn_w_up"][i]
    w["wupg"] = panels(up[:, :F])
    w["wupv"] = panels(up[:, F:])
    w["wdn"] = panels(I["ffn_w_down"][i])
    cw = I["ffn_conv_w"][i]
    w["convw"] = np.ascontiguousarray(cw.reshape(3, 2, FC, 128).transpose(3, 2, 1, 0))
    w["convb"] = np.ascontiguousarray(I["ffn_conv_b"][i].reshape(2, FC, 128).transpose(2, 1, 0))
    w["lng"] = np.ascontiguousarray(I["ln_g"][i].reshape(3, KC, 128).transpose(2, 0, 1))
    w["lnb"] = np.ascontiguousarray(I["ln_b"][i].reshape(3, KC, 128).transpose(2, 0, 1))
    return w


def run_pm(I, layer, mixer_wo, mix_bf, xflat):
    w = prep_pm_weights(I, layer, mixer_wo)
    mem = I["mem"]
    maps = []
    for c in range(NCORE):
        b, t0 = c // 4, (c % 4) * 2048
        g0 = b * SEQ + t0
        if t0 == 0:
            mrows = np.concatenate([np.zeros((2, D), mix_bf.dtype), mix_bf[g0:g0 + 2048]], 0)
            xrows = np.concatenate([np.zeros((2, D), np.float32), xflat[g0:g0 + 2048]], 0)
        else:
            mrows = mix_bf[g0 - 2:g0 + 2048]
            xrows = xflat[g0 - 2:g0 + 2048]
        m = dict(w)
        m["mixT"] = fm(mrows)
        m["xT"] = fm(xrows)
        m["memT"] = fm(mem[b])
        m["hmask"] = np.full((128, 1), 0.0 if t0 == 0 else 1.0, np.float32)
        maps.append(m)
    nc = build_pm(NT)
    res = run_bass_kernel_spmd(nc, maps, core_ids=list(range(NCORE)))
    out = np.empty((2 * SEQ, D), np.float32)
    for c in range(NCORE):
        b, t0 = c // 4, (c % 4) * 2048
        g0 = b * SEQ + t0
        yT = np.asarray(res.results[c]["yT"])
        out[g0:g0 + 2048] = yT.transpose(2, 1, 0).reshape(NCOL, D)[2:]
    return out


def att_inputs(I, xT, c):
    wq = I["da_w_qkv"][0]
    m = {"xT": xT}
    ws = []
    for hh in range(2):
        h = 2 * c + hh
        cols = np.concatenate([np.arange(h * 256, (h + 1) * 256), 4096 + np.arange(h * 256, (h + 1) * 256),
                               8192 + np.arange(h * 256, (h + 1) * 256)])
        ws.append(wq[:, cols].reshape(KC, 128, 768).transpose(1, 0, 2))
    m["wqkv"] = np.ascontiguousarray(np.stack(ws))
    rb = I["rel_bias"][:, 2 * c:2 * c + 2]
    m["relb"] = np.ascontiguousarray(np.broadcast_to(rb[None], (128, 32, 2)))
    m["lam"] = np.ascontiguousarray(np.broadcast_to(I["da_lam"][0][None], (128, 4, 128)))
    m["subw"] = np.ascontiguousarray(np.broadcast_to(I["da_subln"][0][None], (128, 256)))
    k = np.arange(128)[:, None]
    j = np.arange(512)[None, :]
    m["nidx"] = (j - k - 128).astype(np.float32)
    return m


def r1_inputs(I, x1, c):
    B, S, _ = x1.shape
    cols = slice(c * 512, (c + 1) * 512)
    m = {}
    m["mix"] = np.ascontiguousarray(I["rw_mix"][0].reshape(6, KC, 128).transpose(2, 0, 1))
    rkv = I["rw_w_rkv"][0]
    m["wr"] = tile_k(rkv[0][:, cols])
    m["wk"] = tile_k(rkv[1][:, cols])
    m["wv"] = tile_k(rkv[2][:, cols])
    m["w2s"] = np.ascontiguousarray(I["rw_w2"][0][:, cols])
    m["a2s"] = np.ascontiguousarray(I["rw_a2"][0][:, cols])
    m["g2s"] = np.ascontiguousarray(I["rw_g2"][0][:, cols].reshape(4, 120, 512).transpose(1, 0, 2))
    m["w0"] = rep(I["rw_w0"][0][cols])
    m["a0"] = rep(I["rw_a0"][0][cols])
    m["kkv"] = rep(I["rw_k_k"][0][cols])
    m["kav"] = rep(I["rw_k_a"][0][cols])
    m["rkv"] = rep(I["rw_r_k"][0].reshape(-1)[cols])
    return m


def kernel(**I):
    I = {k: np.asarray(v) for k, v in I.items()}
    x = I["x"].astype(np.float32, copy=False)
    xflat = x.reshape(2 * SEQ, D)
    xT = fm(xflat)
    nc = build_att(SEQ, 2)
    res = run_bass_kernel_spmd(nc, [att_inputs(I, xT, c) for c in range(NCORE)], core_ids=list(range(NCORE)))
    mix0 = np.empty((2 * SEQ, D), ml_dtypes.bfloat16)
    for c in range(NCORE):
        o = np.asarray(res.results[c]["o"])
        for hh in range(2):
            h = 2 * c + hh
            mix0[:, h * 256:(h + 1) * 256] = o[hh]
    del xT, res
    x1 = run_pm(I, 0, I["da_w_o"][0], mix0, xflat)
    x1b = x1.reshape(2, SEQ, D)
    parts = []
    for b in range(2):
        xp = np.concatenate([np.zeros((1, D), np.float32), x1b[b]], 0)
        parts.append(xp.T.reshape(KC, 128, SEQ + 1).transpose(1, 0, 2))
    x1T = np.ascontiguousarray(np.concatenate(parts, 2))
    shared = {"x1T": x1T, "w1": tile_k(I["rw_w1"][0]), "a1": tile_k(I["rw_a1"][0]), "g1": tile_k(I["rw_g1"][0])}
    maps = []
    for c in range(NCORE):
        m = r1_inputs(I, x1b, c)
        m.update(shared)
        maps.append(m)
    for c in range(NCORE):
        cols = slice(c * 512, (c + 1) * 512)
        maps[c]["lnw"] = rep(I["rw_lnx_w"][0][cols])
        maps[c]["lnb"] = rep(I["rw_lnx_b"][0][cols])
    nc = build_rw(SEQ, 2)
    res2 = run_bass_kernel_spmd(nc, maps, core_ids=list(range(NCORE)))
    mix1 = np.empty((2 * SEQ, D), ml_dtypes.bfloat16)
    for c in range(NCORE):
        mix1[:, c * 512:(c + 1) * 512] = np.asarray(res2.results[c]["o"])
    del maps, res2, x1T, shared
    out = run_pm(I, 1, I["rw_w_o"][0], mix1, x1)
    return out.reshape(2, SEQ, D)
```
